# Optimizing a Trainium2 kernel written in Bass

```python
import math
import jax, jax.numpy as jnp
from jax import lax
import numpy as np

D_MODEL = 1024
BATCH = 16
SEQ = 2048
DEPTH = 4
DEC_BATCH = 16
DEC_SEQ = 4096
PAST_LEN = 128

N_MIXERS = 4
GRID_W = 64
EPS = 1e-6
ROPE_THETA = 10000.0
DA_HEAD_DIM = 64
DA_HEADS = D_MODEL // (2 * DA_HEAD_DIM)
Q_BLOCK = 128
S5_GROUP = 16
S5_GROUPS = D_MODEL // S5_GROUP
S5_STATE = 64
NA_HEAD_DIM = 64
NA_HEADS = D_MODEL // NA_HEAD_DIM
NA_WIN_ROWS = 8
NA_WIN_COLS = 16
DN_HEAD_DIM = 128
DN_HEADS = D_MODEL // DN_HEAD_DIM
DN_CONV = 4
DN_CHUNK = 64
FFN_HIDDEN = -(-8 * D_MODEL // (3 * 256)) * 256

kernel_name = 'hybrid_bidir_encoder'


def rms_norm(x, g):
    xf = x.astype(jnp.float32)
    y = xf * lax.rsqrt(jnp.mean(xf * xf, axis=-1, keepdims=True) + EPS)
    return (y * g.astype(jnp.float32)).astype(x.dtype)


def apply_rope(x):
    l, d = x.shape[1], x.shape[-1]
    half = d // 2
    inv = jnp.power(ROPE_THETA, -jnp.arange(half, dtype=jnp.float32) * 2.0 / d)
    ang = jnp.arange(l, dtype=jnp.float32)[:, None] * inv[None, :]
    shape = (1, l) + (1,) * (x.ndim - 3) + (half,)
    cos = jnp.cos(ang).reshape(shape)
    sin = jnp.sin(ang).reshape(shape)
    xf = x.astype(jnp.float32)
    x1, x2 = xf[..., :half], xf[..., half:]
    return jnp.concatenate([x1 * cos - x2 * sin, x2 * cos + x1 * sin], axis=-1).astype(x.dtype)


def swiglu(h, w1, w3, w2):
    return (jax.nn.silu(h @ w1) * (h @ w3)) @ w2


def diff_attention(h, w_in, lam, subln_g, w_out, lambda_init):
    b, l, _ = h.shape
    q, k, v = jnp.split(h @ w_in, 3, axis=-1)
    q = apply_rope(q.reshape(b, l, DA_HEADS, 2, DA_HEAD_DIM))
    k = apply_rope(k.reshape(b, l, DA_HEADS, 2, DA_HEAD_DIM))
    v = v.reshape(b, l, DA_HEADS, 2 * DA_HEAD_DIM)
    lam = lam.astype(jnp.float32)
    lam_full = jnp.exp(jnp.sum(lam[0] * lam[1])) - jnp.exp(jnp.sum(lam[2] * lam[3])) + lambda_init
    scale = DA_HEAD_DIM ** -0.5
    n_blk = l // Q_BLOCK
    q_blocks = q.reshape(b, n_blk, Q_BLOCK, DA_HEADS, 2, DA_HEAD_DIM).transpose(1, 0, 2, 3, 4, 5)

    def block(q_blk):
        s = jnp.einsum('bqhcd,bkhcd->bhcqk', q_blk, k).astype(jnp.float32) * scale
        p = jax.nn.softmax(s, axis=-1)
        a = p[:, :, 0] - lam_full * p[:, :, 1]
        return jnp.einsum('bhqk,bkhe->bqhe', a.astype(v.dtype), v)

    o = lax.map(block, q_blocks)
    o = o.transpose(1, 0, 2, 3, 4).reshape(b, l, DA_HEADS, 2 * DA_HEAD_DIM)
    o = rms_norm(o, subln_g) * (1.0 - lambda_init)
    return o.reshape(b, l, D_MODEL) @ w_out


def _recurrence_combine(left, right):
    a_l, b_l = left
    a_r, b_r = right
    return a_r * a_l, a_r * b_l + b_r


def s5_mixer(h, w_in, a_re, a_im, log_dt, b_re, b_im, c_re, c_im, d_skip, w_glu):
    b, l, _ = h.shape
    f32 = jnp.float32
    u = h @ w_in
    lam = lax.complex(a_re.astype(f32), a_im.astype(f32))
    dt = jnp.exp(log_dt.astype(f32))[..., None]
    lam_bar = jnp.exp(lam * dt)
    b_mat = lax.complex(b_re.astype(f32), b_im.astype(f32))
    b_bar = ((lam_bar - 1.0) / lam)[..., None] * b_mat
    c_mat = lax.complex(c_re.astype(f32), c_im.astype(f32))

    def one_sequence(us):
        usc = us.astype(jnp.complex64)

        def direction(dr, reverse):
            bu = jnp.einsum('lgp,gnp->lgn', usc, b_bar[dr])
            a = jnp.broadcast_to(lam_bar[dr], bu.shape)
            _, states = lax.associative_scan(_recurrence_combine, (a, bu), reverse=reverse, axis=0)
            return jnp.real(jnp.einsum('gpn,lgn->lgp', c_mat[dr], states))

        return direction(0, False) + direction(1, True)

    y = lax.map(one_sequence, u.astype(f32).reshape(b, l, S5_GROUPS, S5_GROUP))
    y = y.reshape(b, l, D_MODEL) + d_skip.astype(f32) * u.astype(f32)
    gl, gt = jnp.split(jax.nn.gelu(y).astype(h.dtype) @ w_glu, 2, axis=-1)
    return gl * jax.nn.sigmoid(gt)


def neighborhood_attention(h, w_in, rpb, w_out):
    b, l, _ = h.shape
    rows = l // GRID_W
    kr = min(NA_WIN_ROWS, rows)
    kc = NA_WIN_COLS
    q, k, v = [t.reshape(b, rows, GRID_W, NA_HEADS, NA_HEAD_DIM) for t in jnp.split(h @ w_in, 3, axis=-1)]
    row_ids = jnp.arange(rows)
    row_start = jnp.clip(row_ids - kr // 2, 0, rows - kr)
    cols = jnp.arange(GRID_W)
    col_start = jnp.clip(cols - kc // 2, 0, GRID_W - kc)
    col_valid = (cols[None, :] >= col_start[:, None]) & (cols[None, :] < col_start[:, None] + kc)
    col_bias_idx = jnp.clip(cols[None, :] - cols[:, None], -(kc - 1), kc - 1) + NA_WIN_COLS - 1
    scale = NA_HEAD_DIM ** -0.5

    def row_block(args):
        q_r, r, rs = args
        k_band = lax.dynamic_slice_in_dim(k, rs, kr, axis=1)
        v_band = lax.dynamic_slice_in_dim(v, rs, kr, axis=1)
        row_bias_idx = rs + jnp.arange(kr) - r + NA_WIN_ROWS - 1
        bias = rpb[:, row_bias_idx[None, :, None], col_bias_idx[:, None, :]]
        s = jnp.einsum('bqhd,bikhd->bhqik', q_r, k_band).astype(jnp.float32) * scale
        s = s + bias.astype(jnp.float32)[None]
        s = jnp.where(col_valid[:, None, :], s, -1e30)
        p = jax.nn.softmax(s.reshape(b, NA_HEADS, GRID_W, kr * GRID_W), axis=-1)
        p = p.reshape(b, NA_HEADS, GRID_W, kr, GRID_W)
        return jnp.einsum('bhqik,bikhd->bqhd', p.astype(v.dtype), v_band)

    o = lax.map(row_block, (q.transpose(1, 0, 2, 3, 4), row_ids, row_start))
    o = o.transpose(1, 0, 2, 3, 4).reshape(b, l, D_MODEL)
    return o @ w_out


def centred_depthwise_conv(x, w):
    kk, ch = w.shape
    return lax.conv_general_dilated(x, w[:, None, :], window_strides=(1,),
                                    padding=[((kk - 1) // 2, kk // 2)],
                                    dimension_numbers=('NWC', 'WIO', 'NWC'),
                                    feature_group_count=ch)


def l2_normalize(x):
    return x * lax.rsqrt(jnp.sum(x * x, axis=-1, keepdims=True) + EPS)


def chunk_gated_delta(q, k, v, g, beta):
    b, l, nh, dk = q.shape
    dv = v.shape[-1]
    n = l // DN_CHUNK
    c = DN_CHUNK

    def to_chunks(t):
        return t.reshape(b, n, c, nh, -1).transpose(0, 3, 1, 2, 4)

    q, k, v = to_chunks(q), to_chunks(k), to_chunks(v)
    g = g.reshape(b, n, c, nh).transpose(0, 3, 1, 2)
    beta = beta.reshape(b, n, c, nh).transpose(0, 3, 1, 2)
    gc = jnp.cumsum(g, axis=-1)
    causal = jnp.tril(jnp.ones((c, c), dtype=bool))
    strict = jnp.tril(jnp.ones((c, c), dtype=bool), k=-1)
    diff = gc[..., :, None] - gc[..., None, :]
    decay = jnp.where(causal, jnp.exp(jnp.where(causal, diff, 0.0)), 0.0)
    kb = k * beta[..., None]
    m = jnp.where(strict, jnp.einsum('bhncd,bhnsd->bhncs', kb, k) * decay, 0.0)
    rhs = jnp.concatenate([v * beta[..., None], kb * jnp.exp(gc)[..., None]], axis=-1)
    sol = lax.linalg.triangular_solve(jnp.eye(c, dtype=jnp.float32) + m, rhs,
                                      left_side=True, lower=True, unit_diagonal=True)
    u, w = sol[..., :dv], sol[..., dv:]
    attn = jnp.einsum('bhncd,bhnsd->bhncs', q, k) * decay
    q_dec = q * jnp.exp(gc)[..., None]
    k_dec = k * jnp.exp(gc[..., -1:] - gc)[..., None]
    chunk_decay = jnp.exp(gc[..., -1])

    def step(state, xs):
        u_i, w_i, qd_i, kd_i, attn_i, cd_i = xs
        v_new = u_i - jnp.einsum('bhcd,bhde->bhce', w_i, state)
        o = jnp.einsum('bhcd,bhde->bhce', qd_i, state) + jnp.einsum('bhcs,bhse->bhce', attn_i, v_new)
        state = state * cd_i[..., None, None] + jnp.einsum('bhcd,bhce->bhde', kd_i, v_new)
        return state, o

    xs = tuple(jnp.moveaxis(t, 2, 0) for t in (u, w, q_dec, k_dec, attn, chunk_decay))
    s0 = jnp.zeros((b, nh, dk, dv), jnp.float32)
    _, o = lax.scan(step, s0, xs)
    return o.transpose(1, 0, 3, 2, 4).reshape(b, l, nh, dv)


def gated_deltanet(h, w_in, conv_w, a_log, dt_bias, onorm_g, w_out):
    b, l, _ = h.shape
    f32 = jnp.float32
    proj = h @ w_in
    qkv = jax.nn.silu(centred_depthwise_conv(proj[..., :3 * D_MODEL], conv_w))
    z = proj[..., 3 * D_MODEL:4 * D_MODEL]
    ab = proj[..., 4 * D_MODEL:].reshape(b, l, 4, DN_HEADS).astype(f32)
    q, k, v = [t.reshape(b, l, DN_HEADS, DN_HEAD_DIM).astype(f32) for t in jnp.split(qkv, 3, axis=-1)]
    q = l2_normalize(q) * (DN_HEAD_DIM ** -0.5)
    k = l2_normalize(k)
    a_log = a_log.astype(f32)
    dt_bias = dt_bias.astype(f32)
    g_f = -jnp.exp(a_log[0]) * jax.nn.softplus(ab[:, :, 0] + dt_bias[0])
    g_b = -jnp.exp(a_log[1]) * jax.nn.softplus(ab[:, :, 1] + dt_bias[1])
    beta_f = jax.nn.sigmoid(ab[:, :, 2])
    beta_b = jax.nn.sigmoid(ab[:, :, 3])
    o_f = chunk_gated_delta(q, k, v, g_f, beta_f)
    flip = lambda t: jnp.flip(t, axis=1)
    o_b = flip(chunk_gated_delta(flip(q), flip(k), flip(v), flip(g_b), flip(beta_b)))
    o = rms_norm(o_f + o_b, onorm_g) * jax.nn.silu(z.reshape(b, l, DN_HEADS, DN_HEAD_DIM).astype(f32))
    return o.astype(h.dtype).reshape(b, l, D_MODEL) @ w_out


def encoder_trunk(x, c, p):
    for i in range(DEPTH):
        m, j = i % N_MIXERS, i // N_MIXERS
        mod = (jax.nn.silu(c) @ p['ada_w'][i] + p['ada_b'][i])[:, None, :]
        sh1, sc1, g1, sh2, sc2, g2 = jnp.split(mod, 6, axis=-1)
        hmix = rms_norm(x, p['norm1_g'][i]) * (1.0 + sc1) + sh1
        if m == 0:
            lambda_init = 0.8 - 0.6 * math.exp(-0.3 * i)
            y = diff_attention(hmix, p['da_w_in'][j], p['da_lam'][j], p['da_subln_g'][j],
                               p['da_w_out'][j], lambda_init)
        elif m == 1:
            y = s5_mixer(hmix, p['s5_w_in'][j], p['s5_a_re'][j], p['s5_a_im'][j], p['s5_log_dt'][j],
                         p['s5_b_re'][j], p['s5_b_im'][j], p['s5_c_re'][j], p['s5_c_im'][j],
                         p['s5_d'][j], p['s5_w_glu'][j])
        elif m == 2:
            y = neighborhood_attention(hmix, p['na_w_in'][j], p['na_rpb'][j], p['na_w_out'][j])
        else:
            y = gated_deltanet(hmix, p['dn_w_in'][j], p['dn_conv_w'][j], p['dn_a_log'][j],
                               p['dn_dt_bias'][j], p['dn_onorm_g'][j], p['dn_w_out'][j])
        x = x + g1 * y
        hffn = rms_norm(x, p['norm2_g'][i]) * (1.0 + sc2) + sh2
        x = x + g2 * swiglu(hffn, p['ffn_w1'][i], p['ffn_w3'][i], p['ffn_w2'][i])
    return rms_norm(x, p['final_g'])


def setup_inputs(seed: int = 0) -> dict:
    key = jax.random.key(seed)
    ks = iter(jax.random.split(key, 64))
    f32 = jnp.float32

    def nrm(shape, scale):
        return jax.random.normal(next(ks), shape, f32) * scale

    def unif(shape, lo, hi):
        return jax.random.uniform(next(ks), shape, f32, lo, hi)

    n_a, n_s, n_na, n_dn = [len(range(m, DEPTH, N_MIXERS)) for m in range(N_MIXERS)]
    d, f = D_MODEL, FFN_HIDDEN
    g, pp, ns = S5_GROUPS, S5_GROUP, S5_STATE
    x_prompt = nrm((BATCH, SEQ, d), 1.0)
    x_sample = nrm((DEC_BATCH, DEC_SEQ, d), 1.0)
    c_prompt = nrm((BATCH, d), 1.0)
    c_sample = nrm((DEC_BATCH, d), 1.0)
    ada_w = nrm((DEPTH, d, 6 * d), 0.5 * d ** -0.5)
    ada_b = nrm((DEPTH, 6 * d), 0.02)
    norm1_g = 1.0 + nrm((DEPTH, d), 0.02)
    norm2_g = 1.0 + nrm((DEPTH, d), 0.02)
    ffn_w1 = nrm((DEPTH, d, f), d ** -0.5)
    ffn_w3 = nrm((DEPTH, d, f), d ** -0.5)
    ffn_w2 = nrm((DEPTH, f, d), f ** -0.5)
    da_w_in = nrm((n_a, d, 3 * d), d ** -0.5)
    da_lam = nrm((n_a, 4, DA_HEAD_DIM), 0.1)
    da_subln_g = 1.0 + nrm((n_a, 2 * DA_HEAD_DIM), 0.02)
    da_w_out = nrm((n_a, d, d), d ** -0.5)
    s5_w_in = nrm((n_s, d, d), d ** -0.5)
    s5_a_re = -0.5 + nrm((n_s, 2, g, ns), 0.01)
    s5_a_im = math.pi * jnp.arange(ns, dtype=f32) + nrm((n_s, 2, g, ns), 0.01)
    s5_log_dt = unif((n_s, 2, g), math.log(1e-3), math.log(1e-1))
    s5_b_re = nrm((n_s, 2, g, ns, pp), (2 * pp) ** -0.5)
    s5_b_im = nrm((n_s, 2, g, ns, pp), (2 * pp) ** -0.5)
    s5_c_re = nrm((n_s, 2, g, pp, ns), (2 * ns) ** -0.5)
    s5_c_im = nrm((n_s, 2, g, pp, ns), (2 * ns) ** -0.5)
    s5_d = nrm((n_s, d), 1.0)
    s5_w_glu = nrm((n_s, d, 2 * d), d ** -0.5)
    na_w_in = nrm((n_na, d, 3 * d), d ** -0.5)
    na_rpb = nrm((n_na, NA_HEADS, 2 * NA_WIN_ROWS - 1, 2 * NA_WIN_COLS - 1), 0.1)
    na_w_out = nrm((n_na, d, d), d ** -0.5)
    dn_w_in = nrm((n_dn, d, 4 * d + 4 * DN_HEADS), d ** -0.5)
    dn_conv_w = nrm((n_dn, DN_CONV, 3 * d), DN_CONV ** -0.5)
    dn_a_log = jnp.log(unif((n_dn, 2, DN_HEADS), 1.0, 16.0))
    dt = jnp.exp(unif((n_dn, 2, DN_HEADS), math.log(1e-3), math.log(1e-1)))
    dn_dt_bias = dt + jnp.log(-jnp.expm1(-dt))
    dn_onorm_g = 1.0 + nrm((n_dn, DN_HEAD_DIM), 0.02)
    dn_w_out = nrm((n_dn, d, d), d ** -0.5)
    final_g = 1.0 + nrm((d,), 0.02)
    return {
        'x_prompt': x_prompt, 'x_sample': x_sample, 'c_prompt': c_prompt, 'c_sample': c_sample,
        'ada_w': ada_w, 'ada_b': ada_b, 'norm1_g': norm1_g, 'norm2_g': norm2_g,
        'ffn_w1': ffn_w1, 'ffn_w3': ffn_w3, 'ffn_w2': ffn_w2,
        'da_w_in': da_w_in, 'da_lam': da_lam, 'da_subln_g': da_subln_g, 'da_w_out': da_w_out,
        's5_w_in': s5_w_in, 's5_a_re': s5_a_re, 's5_a_im': s5_a_im, 's5_log_dt': s5_log_dt,
        's5_b_re': s5_b_re, 's5_b_im': s5_b_im, 's5_c_re': s5_c_re, 's5_c_im': s5_c_im,
        's5_d': s5_d, 's5_w_glu': s5_w_glu,
        'na_w_in': na_w_in, 'na_rpb': na_rpb, 'na_w_out': na_w_out,
        'dn_w_in': dn_w_in, 'dn_conv_w': dn_conv_w, 'dn_a_log': dn_a_log, 'dn_dt_bias': dn_dt_bias,
        'dn_onorm_g': dn_onorm_g, 'dn_w_out': dn_w_out,
        'final_g': final_g,
    }


def reference(x_prompt, x_sample, c_prompt, c_sample,
              ada_w, ada_b, norm1_g, norm2_g, ffn_w1, ffn_w3, ffn_w2,
              da_w_in, da_lam, da_subln_g, da_w_out,
              s5_w_in, s5_a_re, s5_a_im, s5_log_dt, s5_b_re, s5_b_im, s5_c_re, s5_c_im, s5_d, s5_w_glu,
              na_w_in, na_rpb, na_w_out,
              dn_w_in, dn_conv_w, dn_a_log, dn_dt_bias, dn_onorm_g, dn_w_out,
              final_g):
    params = dict(
        ada_w=ada_w, ada_b=ada_b, norm1_g=norm1_g, norm2_g=norm2_g,
        ffn_w1=ffn_w1, ffn_w3=ffn_w3, ffn_w2=ffn_w2,
        da_w_in=da_w_in, da_lam=da_lam, da_subln_g=da_subln_g, da_w_out=da_w_out,
        s5_w_in=s5_w_in, s5_a_re=s5_a_re, s5_a_im=s5_a_im, s5_log_dt=s5_log_dt,
        s5_b_re=s5_b_re, s5_b_im=s5_b_im, s5_c_re=s5_c_re, s5_c_im=s5_c_im,
        s5_d=s5_d, s5_w_glu=s5_w_glu,
        na_w_in=na_w_in, na_rpb=na_rpb, na_w_out=na_w_out,
        dn_w_in=dn_w_in, dn_conv_w=dn_conv_w, dn_a_log=dn_a_log, dn_dt_bias=dn_dt_bias,
        dn_onorm_g=dn_onorm_g, dn_w_out=dn_w_out,
        final_g=final_g,
    )
    y_prompt = encoder_trunk(x_prompt, c_prompt, params)
    y_sample = encoder_trunk(x_sample, c_sample, params)
    return (y_prompt, y_sample)
```

```python
import math
from contextlib import ExitStack

import numpy as np
import concourse.bass as bass
import concourse.mybir as mybir
from concourse.bass_utils import run_bass_kernel_spmd

F32 = mybir.dt.float32
BF16 = mybir.dt.bfloat16
AF = mybir.ActivationFunctionType
ALU = mybir.AluOpType
AX = mybir.AxisListType

RANGE = 16000
NDSEM = 6
SAME_ENGINE_SYNC = True


class V:
    __slots__ = ("t", "ap")

    def __init__(self, t, ap):
        self.t = t
        self.ap = ap

    def __getitem__(self, idx):
        return V(self.t, self.ap[idx])

    def re(self, s, **kw):
        return V(self.t, self.ap.rearrange(s, **kw))

    def bc(self, shape):
        return V(self.t, self.ap.to_broadcast(list(shape)))


class T:
    __slots__ = ("h", "w", "r", "name")

    def __init__(self, h, name=""):
        self.h = h
        self.w = {}
        self.r = {}
        self.name = name

    def __getitem__(self, idx):
        return V(self, self.h[idx])

    @property
    def v(self):
        return V(self, self.h[:])


class KB:
    def __init__(self, nc, stack):
        self.nc = nc
        self.stack = stack
        self.engs = {"pe": nc.tensor, "act": nc.scalar, "dve": nc.vector,
                     "pool": nc.gpsimd, "sp": nc.sync}
        self.cnt = {e: 0 for e in self.engs}
        self.csem = {}
        self.seen = {e: {} for e in self.engs}
        self.dq = {}
        self.dnext = {e: 0 for e in self.engs}
        self.out_tokens = []
        self.ninst = 0
        self.nwait = 0
        self.uid = 0

    def sbuf(self, shape, dt, name, stack=None):
        self.uid += 1
        st = stack if stack is not None else self.stack
        t = st.enter_context(self.nc.sbuf_tensor("%s_%d" % (name, self.uid), list(shape), dt))
        return T(t, name)

    def psum(self, shape, dt, name, stack=None):
        self.uid += 1
        st = stack if stack is not None else self.stack
        t = st.enter_context(self.nc.psum_tensor("%s_%d" % (name, self.uid), list(shape), dt))
        return T(t, name)

    def _sem(self, sk):
        if sk not in self.csem:
            self.csem[sk] = self.stack.enter_context(
                self.nc.semaphore("s_" + "_".join(str(x) for x in sk)))
        return self.csem[sk]

    def _wait(self, eng, sk, v):
        if sk[0] == "c" and sk[1] == eng and (eng == "pe" or not SAME_ENGINE_SYNC):
            return
        if self.seen[eng].get(sk, 0) >= v:
            return
        self.engs[eng].wait_ge(self._sem(sk), v)
        self.seen[eng][sk] = v
        self.nwait += 1

    def _deps(self, eng, reads, writes):
        toks = {}
        for b in reads:
            for sk, v in b.w.items():
                if toks.get(sk, 0) < v:
                    toks[sk] = v
        for b in writes:
            for sk, v in b.w.items():
                if toks.get(sk, 0) < v:
                    toks[sk] = v
            for sk, v in b.r.items():
                if toks.get(sk, 0) < v:
                    toks[sk] = v
        for sk, v in toks.items():
            self._wait(eng, sk, v)

    def _commit(self, sk, v, reads, writes):
        for b in reads:
            if b.r.get(sk, 0) < v:
                b.r[sk] = v
        for b in writes:
            b.w[sk] = v
            b.r = {}

    def _split(self, kwargs, outkeys):
        reads, writes = [], []
        kw = {}
        for k, a in kwargs.items():
            if isinstance(a, V):
                (writes if k in outkeys else reads).append(a.t)
                kw[k] = a.ap
            elif isinstance(a, T):
                (writes if k in outkeys else reads).append(a)
                kw[k] = a.h[:]
            else:
                kw[k] = a
        return kw, reads, writes

    def I(self, eng, meth, _extra_reads=(), _extra_writes=(), **kwargs):
        kw, reads, writes = self._split(kwargs, ("out", "accum_out"))
        reads = list(reads) + list(_extra_reads)
        writes = list(writes) + list(_extra_writes)
        self._deps(eng, reads, writes)
        ins = getattr(self.engs[eng], meth)(**kw)
        idx = self.cnt[eng]
        self.cnt[eng] += 1
        sk = ("c", eng, idx // RANGE)
        v = idx % RANGE + 1
        ins.then_inc(self._sem(sk), 1)
        self._commit(sk, v, reads, writes)
        self.ninst += 1

    def dma(self, q, out, in_, is_output=False, **kw):
        reads, writes = [], []
        if isinstance(out, (V, T)):
            writes.append(out.t if isinstance(out, V) else out)
            out = out.ap if isinstance(out, V) else out.h[:]
        if isinstance(in_, (V, T)):
            reads.append(in_.t if isinstance(in_, V) else in_)
            in_ = in_.ap if isinstance(in_, V) else in_.h[:]
        self._deps(q, reads, writes)
        lst = self.dq.setdefault(q, [])
        if len(lst) < NDSEM:
            lst.append([("d", q, len(lst), 0), 0])
            i = len(lst) - 1
        else:
            i = self.dnext[q] % NDSEM
        self.dnext[q] += 1
        ent = lst[i]
        if (ent[1] + 1) * 16 > RANGE:
            self._wait(q, ent[0], ent[1] * 16)
            ent[0] = ("d", q, i, ent[0][3] + 1)
            ent[1] = 0
        if ent[1] > 0:
            self._wait(q, ent[0], ent[1] * 16)
        ent[1] += 1
        ins = self.engs[q].dma_start(out=out, in_=in_, **kw)
        ins.then_inc(self._sem(ent[0]), 16)
        self._commit(ent[0], ent[1] * 16, reads, writes)
        if is_output:
            self.out_tokens.append((ent[0], ent[1] * 16))
        self.ninst += 1

    def _latest(self):
        toks = []
        for e in self.engs:
            if self.cnt[e] > 0:
                idx = self.cnt[e] - 1
                toks.append((("c", e, idx // RANGE), idx % RANGE + 1))
        for q, lst in self.dq.items():
            for ent in lst:
                if ent[1] > 0:
                    toks.append((ent[0], ent[1] * 16))
        return toks

    def barrier(self):
        toks = self._latest()
        for e in self.engs:
            for sk, v in toks:
                if sk[0] == "c" and sk[1] == e:
                    continue
                self._wait(e, sk, v)

    def finish(self, eng="sp"):
        for sk, v in self.out_tokens:
            self._wait(eng, sk, v)
        for sk, v in self._latest():
            self._wait(eng, sk, v)

    def mm(self, out, lhsT, rhs, start=True, stop=True, nocheck=False):
        if nocheck:
            self.I("pe", "matmul", out=out, lhsT=lhsT, rhs=rhs, start=start, stop=stop, skip_group_check=True)
        else:
            self.I("pe", "matmul", out=out, lhsT=lhsT, rhs=rhs, start=start, stop=stop)

    def transpose(self, out, in_, ident):
        self.I("pe", "transpose", out=out, in_=in_, identity=ident)

    def act(self, out, in_, func, scale=1.0, bias=None, accum_out=None):
        kw = dict(out=out, in_=in_, func=func, scale=scale)
        if bias is not None:
            kw["bias"] = bias
        if accum_out is not None:
            kw["accum_out"] = accum_out
        self.I("act", "activation", **kw)

    def tt(self, eng, out, in0, in1, op):
        self.I(eng, "tensor_tensor", out=out, in0=in0, in1=in1, op=op)

    def ts(self, eng, out, in0, s1, s2=None, op0=ALU.mult, op1=None):
        if s2 is None:
            self.I(eng, "tensor_scalar", out=out, in0=in0, scalar1=s1, scalar2=None, op0=op0)
        else:
            self.I(eng, "tensor_scalar", out=out, in0=in0, scalar1=s1, scalar2=s2, op0=op0, op1=op1)

    def stt(self, eng, out, in0, scalar, in1, op0, op1):
        self.I(eng, "scalar_tensor_tensor", out=out, in0=in0, scalar=scalar, in1=in1, op0=op0, op1=op1)

    def copy(self, eng, out, in_):
        if eng == "act":
            self.I("act", "activation", out=out, in_=in_, func=AF.Copy)
        else:
            self.I(eng, "tensor_copy", out=out, in_=in_)

    def recip(self, out, in_):
        self.I("dve", "reciprocal", out=out, in_=in_)

    def memset(self, eng, out, val):
        t = out.t if isinstance(out, V) else out
        ap = out.ap if isinstance(out, V) else out.h[:]
        self.I(eng, "memset", ap=ap, constant=val, _extra_writes=[t])


def sub(t, idx, name=""):
    return T(t.h[idx], name or t.name)


class Ring:
    def __init__(self, items):
        self.items = items
        self.i = 0

    def next(self):
        x = self.items[self.i % len(self.items)]
        self.i += 1
        return x


D = 1024
FH = 2816
NFC = FH // 128
EPS = 1e-6
ROPE_THETA = 10000.0


class Prog:
    def __init__(self, cfg):
        self.cfg = cfg
        self.seqs = cfg["seqs"]
        self.NS = len(self.seqs)
        self.offs = [sum(self.seqs[:i]) for i in range(self.NS)]
        self.Ltot = sum(self.seqs)
        self.Lmax = max(self.seqs)
        self.layers = cfg["layers"]
        self.nc = bass.Bass("TRN2", target_bir_lowering=False)
        self.din = {}
        self.build()

    def inp(self, name, shape, dt=F32):
        t = self.nc.dram_tensor(name, list(shape), dt, kind="ExternalInput")
        self.din[name] = t
        return t.ap()

    def scratch(self, name, shape, dt):
        if self.cfg.get("dbg"):
            return self.nc.dram_tensor(name, list(shape), dt, kind="ExternalOutput").ap()
        return self.nc.dram_tensor(name, list(shape), dt).ap()

    def build(self):
        nc = self.nc
        NS, Ltot = self.NS, self.Ltot
        kinds = set(k for _, k in self.layers)
        self.xT = self.inp("xT", [8, 128, Ltot])
        self.cT = self.inp("cT", [128, 8, NS])
        self.ada_w = self.inp("ada_w", [4, D, 6 * D])
        self.ada_bT = self.inp("ada_bT", [4, 128, 48])
        self.n1g = self.inp("n1g", [4, 128, 8])
        self.n2g = self.inp("n2g", [4, 128, 8])
        self.fing = self.inp("fing", [128, 8])
        self.ffn_w1 = self.inp("ffn_w1", [4, D, FH])
        self.ffn_w3 = self.inp("ffn_w3", [4, D, FH])
        self.ffn_w2 = self.inp("ffn_w2", [4, FH, D])
        self.c_ident = self.inp("c_ident", [128, 128])
        self.c_masks = self.inp("c_masks", [4, 128, 128])
        if "da" in kinds:
            self.da_w_in = self.inp("da_w_in", [1, D, 3 * D])
            self.da_lam = self.inp("da_lam", [1, 1, 256])
            self.da_subln_g = self.inp("da_subln_g", [1, 1, 128])
            self.da_w_out = self.inp("da_w_out", [1, D, D])
            self.c_rope = self.inp("c_rope", [2, 128, 4096])
        if "s5" in kinds:
            self.s5_w_in = self.inp("s5_w_in", [1, D, D])
            self.s5_w_glu = self.inp("s5_w_glu", [1, D, 2 * D])
            self.s5_dT = self.inp("s5_dT", [128, 8])
            self.s5_aA = self.inp("s5_aA", [2, 2, 128, 8, 64])
            self.s5_dtA = self.inp("s5_dtA", [2, 128, 8])
            self.s5_bA = self.inp("s5_bA", [2, 2, 128, 8, 64])
            self.s5_aB = self.inp("s5_aB", [2, 2, 128, 32])
            self.s5_dtB = self.inp("s5_dtB", [2, 128, 32])
            self.s5_cB = self.inp("s5_cB", [2, 2, 128, 32, 16])
            self.c_rowmask = self.inp("c_rowmask", [128, 8])
        if "na" in kinds:
            self.na_w_in = self.inp("na_w_in", [1, D, 3 * D])
            self.na_w_out = self.inp("na_w_out", [1, D, D])
            self.na_rpbp = self.inp("na_rpbp", [1, 16 * 15 * 31 + 128])
            self.c_namask = self.inp("c_namask", [2, 128, 14, 64])
            self.c_antiI = self.inp("c_antiI", [64, 64])
        if "dn" in kinds:
            self.dn_w_in = self.inp("dn_w_in", [1, D, 4 * D + 32])
            self.dn_convT = self.inp("dn_convT", [128, 24, 4])
            self.dn_alog = self.inp("dn_alog", [1, 16])
            self.dn_dtb = self.inp("dn_dtb", [1, 16])
            self.dn_onorm = self.inp("dn_onorm", [1, 128])
            self.dn_w_out = self.inp("dn_w_out", [1, D, D])
            self.c_lmask = self.inp("c_lmask", [2, 7, 128, 128])
        self.yT = nc.dram_tensor("yT", [8, 128, Ltot], F32, kind="ExternalOutput").ap()
        self.XR = self.scratch("XR", [8, 128, Ltot], F32)
        self.OT = self.scratch("OT", [8, 128, Ltot], BF16)
        if kinds & {"da", "na"}:
            self.QT = self.scratch("QT", [8, 128, Ltot], BF16)
            self.KT = self.scratch("KT", [8, 128, Ltot], BF16)
            self.Vd = self.scratch("Vd", [Ltot, D], BF16)
        if "s5" in kinds:
            self.UT = self.scratch("UT", [8, 128, Ltot], F32)
        if "dn" in kinds:
            self.PRE = self.scratch("PRE", [32, 128, Ltot], F32)
            self.ABd = self.scratch("ABd", [32, Ltot], F32)

        with ExitStack() as st:
            kb = KB(nc, st)
            self.kb = kb
            self.ident = kb.sbuf([128, 128], F32, "ident")
            self.identb = kb.sbuf([128, 128], BF16, "identb")
            self.onesb = kb.sbuf([128, 128], BF16, "onesb")
            self.ones1b = kb.sbuf([128, 128], BF16, "ones1b")
            self.onesf = kb.sbuf([128, 128], F32, "onesf")
            self.epsT = kb.sbuf([128, 1], F32, "epsT")
            self.oneT = kb.sbuf([128, 1], F32, "oneT")
            self.sct = kb.sbuf([128, 8, NS], F32, "sct")
            self.modt = kb.sbuf([128, 48, NS], F32, "modt")
            self.A1 = kb.sbuf([128, 8, NS], F32, "A1")
            self.A2 = kb.sbuf([128, 8, NS], F32, "A2")
            self.gn = kb.sbuf([128, 3, 8], F32, "gn")
            kb.dma("sp", self.ident.v, self.c_ident[:, :])
            kb.dma("pool", self.identb.v, self.c_ident[:, :])
            kb.memset("dve", self.onesb, 1.0 / 1024.0)
            kb.memset("dve", self.ones1b, 1.0)
            kb.memset("dve", self.onesf, 1.0)
            kb.memset("dve", self.epsT, EPS)
            kb.memset("dve", self.oneT, 1.0)
            kb.dma("sp", self.sct.v, self.cT[:, :, :])
            kb.act(self.sct.v, self.sct.v, AF.Silu)
            kb.dma("sp", self.gn[:, 2, :], self.fing[:, :])
            first = True
            for li, kind in self.layers:
                xsrc = self.xT if first else self.XR
                first = False
                self.phase_mod(li)
                if kind != "none":
                    self.phase_pre(li, kind, xsrc)
                    getattr(self, "core_" + kind)(li)
                self.phase_post(li, kind, xsrc)
            self.phase_final(self.xT if first else self.XR)
            kb.finish()
        self.stats = (kb.ninst, kb.nwait)

    def phase_mod(self, li):
        kb, NS = self.kb, self.NS
        with ExitStack() as ph:
            wts = Ring([kb.sbuf([128, 6 * D], F32, "adaw%d" % i, ph) for i in range(2)])
            acc = kb.sbuf([128, 48, NS], F32, "modacc", ph)
            bt = kb.sbuf([128, 48], F32, "adab", ph)
            pss = Ring([kb.psum([128, 48, NS], F32, "psmod%d" % i, ph) for i in range(2)])
            kb.dma("sp", bt.v, self.ada_bT[li])
            kb.dma("sp", self.gn[:, 0, :], self.n1g[li])
            kb.dma("sp", self.gn[:, 1, :], self.n2g[li])
            for k in range(8):
                wt = wts.next()
                kb.dma("sp", wt.v, self.ada_w[li, k * 128:(k + 1) * 128, :])
                ps = pss.next()
                for n in range(48):
                    kb.mm(ps[:, n, :], wt[:, n * 128:(n + 1) * 128], self.sct[:, k, :])
                if k == 0:
                    kb.copy("dve", acc.v, ps.v)
                else:
                    kb.tt("dve", acc.v, acc.v, ps.v, ALU.add)
            for s in range(NS):
                kb.tt("dve", self.modt[:, :, s], acc[:, :, s], bt.v, ALU.add)
                kb.stt("dve", self.A1[:, :, s], self.modt[:, 8:16, s], 1.0, self.gn[:, 0, :], ALU.add, ALU.mult)
                kb.stt("dve", self.A2[:, :, s], self.modt[:, 32:40, s], 1.0, self.gn[:, 1, :], ALU.add, ALU.mult)
            kb.barrier()

    def norm_mod(self, xt, N, A, B, ht, sq, rs, psum_ring, out_f32=False):
        kb = self.kb
        kb.act(sq[:, :, :N], xt[:, :, :N], AF.Square)
        ps = psum_ring.next()
        for c in range(8):
            kb.mm(ps[:, :N], self.onesb.v, sq[:, c, :N], start=(c == 0), stop=(c == 7))
        kb.act(rs[:, :N], ps[:, :N], AF.Sqrt, scale=1.0, bias=self.epsT.v)
        kb.recip(rs[:, :N], rs[:, :N])
        for c in range(8):
            if B is None:
                kb.stt("dve", ht[:, c, :N], xt[:, c, :N], A[:, c:c + 1], rs[:, :N], ALU.mult, ALU.mult)
            else:
                kb.stt("dve", sq[:, c, :N], xt[:, c, :N], A[:, c:c + 1], rs[:, :N], ALU.mult, ALU.mult)
                kb.act(ht[:, c, :N], sq[:, c, :N], AF.Identity, scale=1.0, bias=B[:, c:c + 1])

    def tiles(self, N):
        for s, L in enumerate(self.seqs):
            for t0 in range(0, L, N):
                yield s, self.offs[s] + t0, t0, min(N, L - t0)

    def phase_final(self, xsrc):
        kb = self.kb
        N = 512
        with ExitStack() as ph:
            xts = Ring([kb.sbuf([128, 8, N], F32, "fx%d" % i, ph) for i in range(2)])
            yts = Ring([kb.sbuf([128, 8, N], F32, "fy%d" % i, ph) for i in range(2)])
            sq = kb.sbuf([128, 8, N], BF16, "fsq", ph)
            rs = kb.sbuf([128, N], F32, "frs", ph)
            pss = Ring([kb.psum([128, 512], F32, "fps%d" % i, ph) for i in range(2)])
            for s, g0, t0, n in self.tiles(N):
                xt = xts.next()
                yt = yts.next()
                kb.dma("sp", xt[:, :, :n], xsrc[:, :, g0:g0 + n].rearrange("c p t -> p c t"))
                self.norm_mod(xt, n, self.gn[:, 2, :], None, yt, sq, rs, pss)
                kb.dma("pool", self.yT[:, :, g0:g0 + n].rearrange("c p t -> p c t"), yt[:, :, :n], is_output=True)
            kb.barrier()

    def phase_post(self, li, kind, xsrc):
        kb = self.kb
        N = 256
        with ExitStack() as ph:
            w1 = kb.sbuf([128, 8, FH], BF16, "w1", ph)
            w3 = kb.sbuf([128, 8, FH], BF16, "w3", ph)
            w2 = kb.sbuf([128, NFC, D], BF16, "w2", ph)
            kb.dma("pool", w1.v, self.ffn_w1[li].rearrange("(k p) n -> p k n", p=128))
            kb.dma("pool", w3.v, self.ffn_w3[li].rearrange("(k p) n -> p k n", p=128))
            kb.dma("pool", w2.v, self.ffn_w2[li].rearrange("(k p) n -> p k n", p=128))
            wo = None
            if kind in ("da", "na", "dn"):
                wsrc = {"da": self.da_w_out, "na": self.na_w_out, "dn": self.dn_w_out}[kind] if False else getattr(self, kind + "_w_out")
                wo = kb.sbuf([128, 8, D], BF16, "wo", ph)
                kb.dma("pool", wo.v, wsrc[0].rearrange("(k p) n -> p k n", p=128))
            elif kind == "s5":
                wo = kb.sbuf([128, 8, 2 * D], BF16, "wo", ph)
                kb.dma("pool", wo.v, self.s5_w_glu[0].rearrange("(k p) n -> p k n", p=128))
            nb = 1 if kind == "s5" else 2
            xts = Ring([kb.sbuf([128, 8, N], F32, "px%d" % i, ph) for i in range(nb)])
            ots = Ring([kb.sbuf([128, 8, N], BF16, "po%d" % i, ph) for i in range(nb)]) if wo is not None else None
            ht = kb.sbuf([128, 8, N], BF16, "ph", ph)
            sq = kb.sbuf([128, 8, N], BF16, "psq", ph)
            gt = kb.sbuf([128, NFC, N], BF16, "pg", ph)
            rs = kb.sbuf([128, N], F32, "prs", ph)
            tmps = Ring([kb.sbuf([128, N], F32, "ptmp%d" % i, ph) for i in range(3)])
            pss = Ring([kb.psum([128, 512], F32, "pps%d" % i, ph) for i in range(8)])
            for s, g0, t0, n in self.tiles(N):
                xt = xts.next()
                kb.dma("sp", xt[:, :, :n], xsrc[:, :, g0:g0 + n].rearrange("c p t -> p c t"))
                G1 = self.modt[:, 16:24, s]
                G2 = self.modt[:, 40:48, s]
                if wo is not None:
                    ot = ots.next()
                    kb.dma("sp", ot[:, :, :n], self.OT[:, :, g0:g0 + n].rearrange("c p t -> p c t"))
                    for m in range(8):
                        ps = pss.next()
                        for k in range(8):
                            kb.mm(ps[:, :n], wo[:, k, m * 128:(m + 1) * 128], ot[:, k, :n], start=(k == 0), stop=(k == 7))
                        if kind == "s5":
                            ps2 = pss.next()
                            for k in range(8):
                                kb.mm(ps2[:, :n], wo[:, k, D + m * 128:D + (m + 1) * 128], ot[:, k, :n], start=(k == 0), stop=(k == 7))
                            tg = tmps.next()
                            kb.act(tg[:, :n], ps2[:, :n], AF.Sigmoid)
                            kb.tt("dve", tg[:, :n], ps[:, :n], tg[:, :n], ALU.mult)
                            kb.stt("dve", xt[:, m, :n], tg[:, :n], G1[:, m:m + 1], xt[:, m, :n], ALU.mult, ALU.add)
                        else:
                            kb.stt("dve", xt[:, m, :n], ps[:, :n], G1[:, m:m + 1], xt[:, m, :n], ALU.mult, ALU.add)
                self.norm_mod(xt, n, self.A2[:, :, s], self.modt[:, 24:32, s], ht, sq, rs, pss)
                for f in range(NFC):
                    pa = pss.next()
                    pb = pss.next()
                    for k in range(8):
                        kb.mm(pa[:, :n], w1[:, k, f * 128:(f + 1) * 128], ht[:, k, :n], start=(k == 0), stop=(k == 7))
                    for k in range(8):
                        kb.mm(pb[:, :n], w3[:, k, f * 128:(f + 1) * 128], ht[:, k, :n], start=(k == 0), stop=(k == 7))
                    tg = tmps.next()
                    kb.act(tg[:, :n], pa[:, :n], AF.Silu)
                    kb.tt("dve", gt[:, f, :n], tg[:, :n], pb[:, :n], ALU.mult)
                for m in range(8):
                    ps = pss.next()
                    for f in range(NFC):
                        kb.mm(ps[:, :n], w2[:, f, m * 128:(m + 1) * 128], gt[:, f, :n], start=(f == 0), stop=(f == NFC - 1))
                    kb.stt("dve", xt[:, m, :n], ps[:, :n], G2[:, m:m + 1], xt[:, m, :n], ALU.mult, ALU.add)
                kb.dma("pool", self.XR[:, :, g0:g0 + n].rearrange("c p t -> p c t"), xt[:, :, :n])
            kb.barrier()


    def phase_pre(self, li, kind, xsrc):
        kb = self.kb
        N = 512
        with ExitStack() as ph:
            xts = Ring([kb.sbuf([128, 8, N], F32, "qx%d" % i, ph) for i in range(2)])
            ht = kb.sbuf([128, 8, N], BF16, "qh", ph)
            sq = kb.sbuf([128, 8, N], BF16, "qsq", ph)
            rs = kb.sbuf([128, N], F32, "qrs", ph)
            pss = Ring([kb.psum([128, 512], F32, "qps%d" % i, ph) for i in range(8)])
            if kind in ("da", "na"):
                wsrc = self.da_w_in if kind == "da" else self.na_w_in
                wq = kb.sbuf([128, 8, D], BF16, "wq", ph)
                wk = kb.sbuf([128, 8, D], BF16, "wk", ph)
                wv = kb.sbuf([128, 8, D], BF16, "wv", ph)
                w3d = wsrc[0].rearrange("(k p) n -> p k n", p=128)
                kb.dma("pool", wq.v, w3d[:, :, 0:D])
                kb.dma("pool", wk.v, w3d[:, :, D:2 * D])
                kb.dma("pool", wv.v, w3d[:, :, 2 * D:3 * D])
                if kind == "da":
                    wq2 = kb.sbuf([128, 8, 16, 2, 32], BF16, "wq2", ph)
                    wk2 = kb.sbuf([128, 8, 16, 2, 32], BF16, "wk2", ph)
                    for w2_, c0 in ((wq2, 0), (wk2, D)):
                        src = w3d[:, :, c0:c0 + D].rearrange("p k (h two d) -> p k h two d", two=2, d=32)
                        for k in range(8):
                            kb.dma("pool", w2_[:, k, :, 0, :], src[:, k, :, 1, :])
                            kb.dma("pool", w2_[:, k, :, 1, :], src[:, k, :, 0, :])
                        kb.ts("dve", w2_[:, :, :, 0, :], w2_[:, :, :, 0, :], -1.0)
                    cst = Ring([kb.sbuf([128, 2, N], F32, "cs%d" % i, ph) for i in range(2)])
                    t1s = Ring([kb.sbuf([128, N], F32, "rt1%d" % i, ph) for i in range(2)])
                    t2s = Ring([kb.sbuf([128, N], F32, "rt2%d" % i, ph) for i in range(2)])
                qo = Ring([kb.sbuf([128, 8, N], BF16, "qo%d" % i, ph) for i in range(2)])
                vo = Ring([kb.sbuf([128, 4, D], BF16, "vo%d" % i, ph) for i in range(2)])
            elif kind == "s5":
                wu = kb.sbuf([128, 8, D], BF16, "wu", ph)
                kb.dma("pool", wu.v, self.s5_w_in[0].rearrange("(k p) n -> p k n", p=128))
                uo = Ring([kb.sbuf([128, 8, N], F32, "uo%d" % i, ph) for i in range(2)])
            elif kind == "dn":
                wd = kb.sbuf([128, 8, 4 * D + 32], BF16, "wd", ph)
                kb.dma("pool", wd.v, self.dn_w_in[0].rearrange("(k p) n -> p k n", p=128))
                po = Ring([kb.sbuf([128, 8, N], F32, "po%d" % i, ph) for i in range(2)])
                abo = Ring([kb.sbuf([32, N], F32, "abo%d" % i, ph) for i in range(2)])
            ev = [0]

            def evac(out, ps):
                ev[0] += 1
                kb.copy("act" if ev[0] % 2 else "dve", out, ps)

            def proj(w, col0, n):
                ps = pss.next()
                for k in range(8):
                    kb.mm(ps[:, :n], w[:, k, col0:col0 + 128], ht[:, k, :n], start=(k == 0), stop=(k == 7))
                return ps

            for s, g0, t0, n in self.tiles(N):
                xt = xts.next()
                kb.dma("sp", xt[:, :, :n], xsrc[:, :, g0:g0 + n].rearrange("c p t -> p c t"))
                self.norm_mod(xt, n, self.A1[:, :, s], self.modt[:, 0:8, s], ht, sq, rs, pss)
                if kind in ("da", "na"):
                    if kind == "da":
                        cs = cst.next()
                        kb.dma("sp", cs[:, :, :n], self.c_rope[:, :, t0:t0 + n].rearrange("a p t -> p a t"))
                    for w, w2_, dst in ((wq, wq2 if kind == "da" else None, self.QT), (wk, wk2 if kind == "da" else None, self.KT)):
                        o = qo.next()
                        for m in range(8):
                            ps = proj(w, m * 128, n)
                            if kind == "da":
                                w2v = V(w2_, w2_.h[:].rearrange("p k h two d -> p k (h two d)"))
                                ps2 = pss.next()
                                for k in range(8):
                                    kb.mm(ps2[:, :n], w2v[:, k, m * 128:(m + 1) * 128], ht[:, k, :n], start=(k == 0), stop=(k == 7))
                                t1 = t1s.next()
                                t2 = t2s.next()
                                kb.tt("dve", t1[:, :n], ps[:, :n], cs[:, 0, :n], ALU.mult)
                                kb.tt("dve", t2[:, :n], ps2[:, :n], cs[:, 1, :n], ALU.mult)
                                kb.tt("pool", o[:, m, :n], t1[:, :n], t2[:, :n], ALU.add)
                            else:
                                evac(o[:, m, :n], ps[:, :n])
                        kb.dma("pool", dst[:, :, g0:g0 + n].rearrange("c p t -> p c t"), o[:, :, :n])
                    v = vo.next()
                    for ts_ in range(n // 128):
                        for hf in range(2):
                            ps = pss.next()
                            for k in range(8):
                                kb.mm(ps.v, ht[:, k, ts_ * 128:(ts_ + 1) * 128], wv[:, k, hf * 512:(hf + 1) * 512], start=(k == 0), stop=(k == 7))
                            evac(v[:, ts_, hf * 512:(hf + 1) * 512], ps.v)
                    kb.dma("pool", self.Vd[g0:g0 + n, :].rearrange("(a p) e -> p a e", p=128), v[:, :n // 128, :])
                elif kind == "s5":
                    o = uo.next()
                    for m in range(8):
                        ps = proj(wu, m * 128, n)
                        evac(o[:, m, :n], ps[:, :n])
                    kb.dma("pool", self.UT[:, :, g0:g0 + n].rearrange("c p t -> p c t"), o[:, :, :n])
                elif kind == "dn":
                    for grp in range(4):
                        o = po.next()
                        for m in range(8):
                            ps = proj(wd, (grp * 8 + m) * 128, n)
                            evac(o[:, m, :n], ps[:, :n])
                        kb.dma("pool", self.PRE[grp * 8:(grp + 1) * 8, :, g0:g0 + n].rearrange("c p t -> p c t"), o[:, :, :n])
                    ps = pss.next()
                    for k in range(8):
                        kb.mm(ps[0:32, :n], wd[:, k, 4 * D:4 * D + 32], ht[:, k, :n], start=(k == 0), stop=(k == 7))
                    ab = abo.next()
                    evac(ab[:, :n], ps[0:32, :n])
                    kb.dma("pool", self.ABd[:, g0:g0 + n], ab[:, :n])
            kb.barrier()


    def core_da(self, li):
        kb = self.kb
        lambda_init = 0.8 - 0.6 * math.exp(-0.3 * li)
        Lm = self.Lmax
        NTm = Lm // 128
        with ExitStack() as ph:
            lamt = kb.sbuf([128, 4, 64], F32, "lamt", ph)
            lsum = kb.sbuf([128, 2], F32, "lsum", ph)
            lscr = kb.sbuf([128, 64], F32, "lscr", ph)
            neglam = kb.sbuf([128, 1], F32, "neglam", ph)
            gsub = kb.sbuf([128, 128], F32, "gsub", ph)
            kb.dma("sp", lamt.v.re("p a b -> p (a b)"), self.da_lam[0, 0].partition_broadcast(128))
            kb.dma("sp", gsub.v, self.da_subln_g[0, 0].partition_broadcast(128))
            for j in range(2):
                kb.tt("dve", lscr.v, lamt[:, 2 * j, :], lamt[:, 2 * j + 1, :], ALU.mult)
                kb.I("dve", "reduce_sum", out=lsum[:, j:j + 1], in_=lscr.v, axis=AX.X)
            kb.act(lsum.v, lsum.v, AF.Exp)
            kb.tt("dve", neglam.v, lsum[:, 1:2], lsum[:, 0:1], ALU.subtract)
            kb.ts("dve", neglam.v, neglam.v, -lambda_init, None, ALU.add)
            kb.ts("dve", gsub.v, gsub.v, 1.0 - lambda_init)
            qhs = Ring([kb.sbuf([128, Lm], BF16, "qh%d" % i, ph) for i in range(2)])
            khs = Ring([kb.sbuf([128, Lm], BF16, "kh%d" % i, ph) for i in range(2)])
            vts = Ring([kb.sbuf([128, NTm, 130], BF16, "vt%d" % i, ph) for i in range(2)])
            for vt in vts.items:
                kb.memset("pool", vt[:, :, 128:129], 1.0)
                kb.memset("pool", vt[:, :, 129:130], 0.0)
            oTs = Ring([kb.sbuf([128, Lm], BF16, "oT%d" % i, ph) for i in range(2)])
            ets = Ring([kb.sbuf([128, 512], BF16, "et%d" % i, ph) for i in range(3)])
            o0 = kb.sbuf([128, 4, 128], F32, "o0", ph)
            ods = Ring([kb.sbuf([128, 128], F32, "od%d" % i, ph) for i in range(2)])
            ons = Ring([kb.sbuf([128, 128], BF16, "on%d" % i, ph) for i in range(2)])
            sqj = kb.sbuf([128, 128], F32, "sqj", ph)
            smalls = Ring([kb.sbuf([128, 4], F32, "sm%d" % i, ph) for i in range(4)])
            psS = Ring([kb.psum([128, 512], F32, "psS%d" % i, ph) for i in range(3)])
            psO = Ring([kb.psum([128, 2, 256], F32, "psO%d" % i, ph) for i in range(4)])
            pstb = kb.psum([128, 8, 128], BF16, "psT", ph)
            psT = Ring([sub(pstb, (slice(None), i, slice(None))) for i in range(8)])
            for s, L in enumerate(self.seqs):
                off = self.offs[s]
                NT = L // 128
                for j in range(8):
                    qh, kh, vt, oT = qhs.next(), khs.next(), vts.next(), oTs.next()
                    kb.dma("sp", qh[:, :L], self.QT[j, :, off:off + L])
                    kb.dma("sp", kh[:, :L], self.KT[j, :, off:off + L])
                    kb.dma("sp", vt[:, :NT, 0:128], self.Vd[off:off + L, j * 128:(j + 1) * 128].rearrange("(a p) e -> p a e", p=128))
                    for qt in range(L // 512):
                        for c in range(2):
                            oa = [psO.next(), psO.next()]
                            def smm(kt_):
                                p_ = psS.next()
                                kb.mm(p_.v, kh[c * 64:(c + 1) * 64, kt_ * 128:(kt_ + 1) * 128],
                                      qh[c * 64:(c + 1) * 64, qt * 512:(qt + 1) * 512])
                                return p_
                            pS_next = smm(0)
                            for kt in range(NT):
                                pS = pS_next
                                if kt + 1 < NT:
                                    pS_next = smm(kt + 1)
                                et = ets.next()
                                kb.act(et.v, pS.v, AF.Exp, scale=0.125)
                                for qs in range(4):
                                    kb.mm(oa[qs // 2][:, qs % 2, 0:130], et[:, qs * 128:(qs + 1) * 128], vt[:, kt, :],
                                          start=(kt == 0 and qs % 2 == 0), stop=(kt == NT - 1), nocheck=True)
                            for qs in range(4):
                                acc = oa[qs // 2]
                                sm = smalls.next()
                                kb.recip(sm[:, 0:1], acc[:, qs % 2, 128:129])
                                if c == 0:
                                    kb.ts("dve", o0[:, qs, :], acc[:, qs % 2, 0:128], sm[:, 0:1])
                                else:
                                    od = ods.next()
                                    on = ons.next()
                                    kb.tt("dve", sm[:, 1:2], sm[:, 0:1], neglam.v, ALU.mult)
                                    kb.stt("dve", od.v, acc[:, qs % 2, 0:128], sm[:, 1:2], o0[:, qs, :], ALU.mult, ALU.add)
                                    kb.act(sqj.v, od.v, AF.Square, accum_out=sm[:, 2:3])
                                    kb.act(sm[:, 3:4], sm[:, 2:3], AF.Sqrt, scale=1.0 / 128.0, bias=self.epsT.v)
                                    kb.recip(sm[:, 3:4], sm[:, 3:4])
                                    kb.stt("dve", on.v, od.v, sm[:, 3:4], gsub.v, ALU.mult, ALU.mult)
                                    pT = psT.next()
                                    kb.transpose(pT.v, on.v, self.identb.v)
                                    kb.copy("act", oT[:, qt * 512 + qs * 128: qt * 512 + (qs + 1) * 128], pT.v)
                    kb.dma("pool", self.OT[j, :, off:off + L], oT[:, :L])
            kb.barrier()


    def core_na(self, li):
        kb = self.kb
        Lm = self.Lmax
        NTm = Lm // 128
        rp = self.din["na_rpbp"]
        with ExitStack() as ph:
            mk = kb.sbuf([128, 2, 14, 64], F32, "namask", ph)
            kb.dma("sp", mk.v, self.c_namask.rearrange("a p d q -> p a d q"))
            BTs = Ring([kb.sbuf([128, 14, 64], F32, "BT%d" % i, ph) for i in range(2)])
            H15s = Ring([kb.sbuf([64, 15, 64], F32, "H15%d" % i, ph) for i in range(2)])
            antiI = kb.sbuf([64, 64], F32, "antiI", ph)
            kb.dma("sp", antiI.v, self.c_antiI[:, :])
            psBias = Ring([kb.psum([128, 512], F32, "npB%d" % i, ph) for i in range(1)])
            qhs = Ring([kb.sbuf([64, Lm], BF16, "nq%d" % i, ph) for i in range(2)])
            khs = Ring([kb.sbuf([64, Lm], BF16, "nk%d" % i, ph) for i in range(2)])
            v0s = Ring([kb.sbuf([128, NTm, 66], BF16, "nv0%d" % i, ph) for i in range(2)])
            v1s = Ring([kb.sbuf([128, NTm, 66], BF16, "nv1%d" % i, ph) for i in range(2)])
            for vt in v0s.items + v1s.items:
                kb.memset("pool", vt[:, :, 64:65], 1.0)
                kb.memset("pool", vt[:, :, 65:66], 0.0)
            onTs = Ring([kb.sbuf([64, Lm], BF16, "nonT%d" % i, ph) for i in range(2)])
            sbs = Ring([kb.sbuf([128, 2, 4, 64], F32, "nsb%d" % i, ph) for i in range(2)])
            ets = Ring([kb.sbuf([128, 2, 4, 64], BF16, "net%d" % i, ph) for i in range(3)])
            onts = Ring([kb.sbuf([128, 64], BF16, "nont%d" % i, ph) for i in range(2)])
            sms = Ring([kb.sbuf([128, 1], F32, "nsm%d" % i, ph) for i in range(3)])
            psS = Ring([kb.psum([128, 2, 4, 64], F32, "npS%d" % i, ph) for i in range(3)])
            psO = Ring([kb.psum([128, 512], F32, "npO%d" % i, ph) for i in range(3)])
            pstb = kb.psum([128, 8, 128], BF16, "npT", ph)
            psT = Ring([sub(pstb, (slice(0, 64), i, slice(None))) for i in range(8)])
            for hd in range(16):
                BT = BTs.next()
                H15 = H15s.next()
                kb.dma("sp", H15.v, bass.AP(rp, 64 + hd * 15 * 31 - 48, [[1, 64], [31, 15], [1, 64]]))
                for g_ in range(2):
                    pB = psBias.next()
                    for dd_ in range(7):
                        d0 = g_ * 7 + dd_
                        kb.mm(pB[:, dd_ * 64:(dd_ + 1) * 64], H15[:, d0:d0 + 2, :].re("p a k -> p (a k)"), antiI.v)
                    kb.copy("dve", BT[:, g_ * 7:(g_ + 1) * 7, :], pB[:, 0:448].re("p (a q) -> p a q", q=64))
                kb.tt("dve", BT.v, BT.v, mk[:, 0], ALU.mult)
                kb.tt("dve", BT.v, BT.v, mk[:, 1], ALU.add)
                cq, rq = hd // 2, (hd % 2) * 64
                for s, L in enumerate(self.seqs):
                    off = self.offs[s]
                    rows, NT = L // 64, L // 128
                    qh, kh, v0, v1, onT = qhs.next(), khs.next(), v0s.next(), v1s.next(), onTs.next()
                    kb.dma("sp", qh[:, :L], self.QT[cq, rq:rq + 64, off:off + L])
                    kb.dma("sp", kh[:, :L], self.KT[cq, rq:rq + 64, off:off + L])
                    kb.dma("sp", v0[:, :NT, 0:64], self.Vd[off:off + L, hd * 64:(hd + 1) * 64].rearrange("(a p) e -> p a e", p=128))
                    kb.dma("sp", v1[:, :NT - 1, 0:64], self.Vd[off + 64:off + L - 64, hd * 64:(hd + 1) * 64].rearrange("(a p) e -> p a e", p=128))
                    def stage_a(r2_):
                        pS_ = psS.next()
                        info_ = []
                        for rr in range(2):
                            r = 2 * r2_ + rr
                            rs_ = min(max(r - 4, 0), rows - 8)
                            b0 = rs_ * 64
                            for i in range(4):
                                kb.mm(pS_[:, rr, i, :], kh[:, b0 + 128 * i:b0 + 128 * (i + 1)], qh[:, r * 64:(r + 1) * 64])
                            info_.append((rs_, rs_ - r + 7))
                        return pS_, info_
                    nxt = stage_a(0)
                    for r2 in range(rows // 2):
                        pS, info = nxt
                        if r2 + 1 < rows // 2:
                            nxt = stage_a(r2 + 1)
                        sb = sbs.next()
                        for rr in range(2):
                            base = info[rr][1]
                            kb.stt("dve", sb[:, rr], pS[:, rr], 0.125, BT[:, base:base + 7:2, :], ALU.mult, ALU.add)
                        et = ets.next()
                        kb.act(et.v, sb.v, AF.Exp)
                        pO = psO.next()
                        for rr in range(2):
                            rs_ = info[rr][0]
                            for i in range(4):
                                vt = v0[:, rs_ // 2 + i, :] if rs_ % 2 == 0 else v1[:, (rs_ - 1) // 2 + i, :]
                                kb.mm(pO[rr * 64:(rr + 1) * 64, 0:66], et[:, rr, i, :], vt, start=(i == 0), stop=(i == 3))
                        sm = sms.next()
                        kb.recip(sm.v, pO[:, 64:65])
                        ont = onts.next()
                        kb.ts("dve", ont.v, pO[:, 0:64], sm.v)
                        pT = psT.next()
                        kb.transpose(pT.v, ont.v, self.identb.v)
                        kb.copy("act", onT[:, r2 * 128:(r2 + 1) * 128], pT.v)
                    kb.dma("pool", self.OT[cq, rq:rq + 64, off:off + L], onT[:, :L])
            kb.barrier()


    def core_s5(self, li):
        kb = self.kb
        SEG = min(512, min(self.seqs))
        NLEV = int(round(math.log2(SEG)))
        Lm = self.Lmax
        PI = math.pi
        with ExitStack() as ph:
            BW = kb.sbuf([128, 2, 8, 4, 2, 128], BF16, "BW", ph)
            CW = kb.sbuf([128, 2, 32, 2, 128], BF16, "CW", ph)
            LP = kb.sbuf([128, 2, NLEV, 32, 3], F32, "LP", ph)
            dT = kb.sbuf([128, 8], F32, "s5d", ph)
            negpi = kb.sbuf([128, 1], F32, "negpi", ph)
            kb.memset("dve", negpi, -PI)
            kb.dma("sp", dT.v, self.s5_dT[:, :])
            kb.memset("pool", CW, 0.0)

            def lambar(st, are, aim, dte, shape, name):
                lr = kb.sbuf(shape, F32, name + "lr", st)
                lim = kb.sbuf(shape, F32, name + "li", st)
                t = kb.sbuf(shape, F32, name + "t", st)
                kb.tt("dve", lr.v, are, dte, ALU.mult)
                kb.tt("dve", lim.v, aim, dte, ALU.mult)
                kb.act(lr.v, lr.v, AF.Exp)
                sn = kb.sbuf(shape, F32, name + "sn", st)
                cs = kb.sbuf(shape, F32, name + "cs", st)
                for dst, shift in ((sn, 0.0), (cs, 0.5 * PI)):
                    kb.ts("dve", t.v, lim.v, shift, None, ALU.add)
                    for _ in range(4):
                        kb.ts("dve", dst.v, t.v, PI, -2 * PI, ALU.is_gt, ALU.mult)
                        kb.tt("dve", t.v, t.v, dst.v, ALU.add)
                    kb.act(dst.v, t.v, AF.Sin)
                kb.tt("dve", cs.v, cs.v, lr.v, ALU.mult)
                kb.tt("dve", sn.v, sn.v, lr.v, ALU.mult)
                return cs, sn

            with ExitStack() as st:
                rmask = kb.sbuf([128, 8], F32, "rmask", st)
                kb.dma("sp", rmask.v, self.c_rowmask[:, :])
                for d in range(2):
                    are = kb.sbuf([128, 8, 64], F32, "are%d" % d, st)
                    aim = kb.sbuf([128, 8, 64], F32, "aim%d" % d, st)
                    bre = kb.sbuf([128, 8, 64], F32, "bre%d" % d, st)
                    bim = kb.sbuf([128, 8, 64], F32, "bim%d" % d, st)
                    dte = kb.sbuf([128, 8, 64], F32, "dte%d" % d, st)
                    dt0 = kb.sbuf([128, 8], F32, "dt0%d" % d, st)
                    kb.dma("sp", are.v, self.s5_aA[0, d])
                    kb.dma("sp", aim.v, self.s5_aA[1, d])
                    kb.dma("sp", bre.v, self.s5_bA[0, d])
                    kb.dma("sp", bim.v, self.s5_bA[1, d])
                    kb.dma("sp", dt0.v, self.s5_dtA[d])
                    kb.act(dt0.v, dt0.v, AF.Exp)
                    for ck in range(8):
                        kb.ts("dve", dte[:, ck, :], self.onesf[:, 0:64], dt0[:, ck:ck + 1])
                    lbr, lbi = lambar(st, are.v, aim.v, dte.v, [128, 8, 64], "A%d" % d)
                    den = kb.sbuf([128, 8, 64], F32, "den%d" % d, st)
                    t1 = kb.sbuf([128, 8, 64], F32, "t1%d" % d, st)
                    fr = kb.sbuf([128, 8, 64], F32, "fr%d" % d, st)
                    fi = kb.sbuf([128, 8, 64], F32, "fi%d" % d, st)
                    kb.ts("dve", lbr.v, lbr.v, -1.0, None, ALU.add)
                    kb.tt("dve", den.v, are.v, are.v, ALU.mult)
                    kb.tt("dve", t1.v, aim.v, aim.v, ALU.mult)
                    kb.tt("dve", den.v, den.v, t1.v, ALU.add)
                    kb.recip(den.v, den.v)
                    kb.tt("dve", fr.v, lbr.v, are.v, ALU.mult)
                    kb.tt("dve", t1.v, lbi.v, aim.v, ALU.mult)
                    kb.tt("dve", fr.v, fr.v, t1.v, ALU.add)
                    kb.tt("dve", fr.v, fr.v, den.v, ALU.mult)
                    kb.tt("dve", fi.v, lbi.v, are.v, ALU.mult)
                    kb.tt("dve", t1.v, lbr.v, aim.v, ALU.mult)
                    kb.tt("dve", fi.v, fi.v, t1.v, ALU.subtract)
                    kb.tt("dve", fi.v, fi.v, den.v, ALU.mult)
                    Br = kb.sbuf([128, 8, 64], F32, "Br%d" % d, st)
                    Bi = kb.sbuf([128, 8, 64], F32, "Bi%d" % d, st)
                    kb.tt("dve", Br.v, fr.v, bre.v, ALU.mult)
                    kb.tt("dve", t1.v, fi.v, bim.v, ALU.mult)
                    kb.tt("dve", Br.v, Br.v, t1.v, ALU.subtract)
                    kb.tt("dve", Bi.v, fr.v, bim.v, ALU.mult)
                    kb.tt("dve", t1.v, fi.v, bre.v, ALU.mult)
                    kb.tt("dve", Bi.v, Bi.v, t1.v, ALU.add)
                    for ck in range(8):
                        for j in range(4):
                            for ri, Bx in enumerate((Br, Bi)):
                                for hf in range(2):
                                    kb.ts("dve", BW[:, d, ck, j, ri, hf * 64:(hf + 1) * 64], Bx[:, ck, :],
                                          rmask[:, 2 * j + hf:2 * j + hf + 1])
                    ar2 = kb.sbuf([128, 32], F32, "ar2%d" % d, st)
                    ai2 = kb.sbuf([128, 32], F32, "ai2%d" % d, st)
                    dt2 = kb.sbuf([128, 32], F32, "dt2%d" % d, st)
                    kb.dma("sp", ar2.v, self.s5_aB[0, d])
                    kb.dma("sp", ai2.v, self.s5_aB[1, d])
                    kb.dma("sp", dt2.v, self.s5_dtB[d])
                    kb.act(dt2.v, dt2.v, AF.Exp)
                    pr, pi_ = lambar(st, ar2.v, ai2.v, dt2.v, [128, 32], "B%d" % d)
                    t2 = kb.sbuf([128, 32], F32, "t2%d" % d, st)
                    t3 = kb.sbuf([128, 32], F32, "t3%d" % d, st)
                    for lev in range(NLEV):
                        kb.copy("dve", LP[:, d, lev, :, 0], pr.v)
                        kb.copy("dve", LP[:, d, lev, :, 1], pi_.v)
                        kb.ts("dve", LP[:, d, lev, :, 2], pi_.v, -1.0)
                        if lev < NLEV - 1:
                            kb.tt("dve", t2.v, pr.v, pr.v, ALU.mult)
                            kb.tt("dve", t3.v, pi_.v, pi_.v, ALU.mult)
                            kb.tt("dve", pi_.v, pr.v, pi_.v, ALU.mult)
                            kb.ts("dve", pi_.v, pi_.v, 2.0)
                            kb.tt("dve", pr.v, t2.v, t3.v, ALU.subtract)
                    for ri in range(2):
                        cb = kb.sbuf([128, 32, 16], F32, "cb%d%d" % (d, ri), st)
                        kb.dma("sp", cb.v, self.s5_cB[ri, d])
                        sgn = 1.0 if ri == 0 else -1.0
                        for j in range(4):
                            kb.act(CW[0:64, d, j::4, ri, 32 * j:32 * j + 16], cb[0:64, j::4, :], AF.Copy, scale=sgn)
                            kb.act(CW[64:128, d, j::4, ri, 32 * j + 16:32 * j + 32], cb[64:128, j::4, :], AF.Copy, scale=sgn)
                kb.barrier()

            ufs = Ring([kb.sbuf([128, Lm], F32, "uf%d" % i, ph) for i in range(1)])
            ubs = Ring([kb.sbuf([128, Lm], BF16, "ub%d" % i, ph) for i in range(1)])
            ych = kb.sbuf([128, Lm], F32, "ych", ph)
            gos = Ring([kb.sbuf([128, Lm], BF16, "go%d" % i, ph) for i in range(1)])
            PAD = SEG // 2
            sets = Ring([[kb.sbuf([128, SEG + 2 * PAD], F32, "st%d_%d" % (i, q), ph) for q in range(4)] for i in range(4)])
            for st_ in sets.items:
                for tl_ in st_:
                    kb.memset("pool", tl_, 0.0)
            xbs = Ring([[kb.sbuf([128, SEG], BF16, "xb%d_%d" % (i, q), ph) for q in range(2)] for i in range(4)])
            carries = [[kb.sbuf([128, 2], F32, "car%d_%d" % (d, j), ph) for j in range(4)] for d in range(2)]
            psB = Ring([kb.psum([128, 512], F32, "s5pB%d" % i, ph) for i in range(4)])
            psY = Ring([kb.psum([128, 512], F32, "s5pY%d" % i, ph) for i in range(4)])
            NT5 = SEG // 512
            unit = [0]
            for s, L in enumerate(self.seqs):
                off = self.offs[s]
                nseg = L // SEG
                for ck in range(8):
                    uf, ub, go = ufs.next(), ubs.next(), gos.next()
                    kb.dma("sp", uf[:, :L], self.UT[ck, :, off:off + L])
                    kb.copy("act", ub[:, :L], uf[:, :L])
                    for d in range(2):
                        order = range(nseg) if d == 0 else range(nseg - 1, -1, -1)
                        for si, seg in enumerate(order):
                            t0 = seg * SEG
                            pys = [psY.next() for _ in range(NT5)]
                            for jj in (0, 2):
                                units = []
                                for j in (jj, jj + 1):
                                    gp = ck * 4 + j
                                    Are, Aim, Bre_, Bim_ = sets.next()
                                    xbr, xbi = xbs.next()
                                    car = carries[d][j]
                                    for tt_ in range(NT5):
                                        for ri, dst in enumerate((Are, Aim)):
                                            pb = psB.next()
                                            kb.mm(pb.v, BW[:, d, ck, j, ri, :], ub[:, t0 + tt_ * 512:t0 + (tt_ + 1) * 512])
                                            kb.copy("act", dst[:, PAD + tt_ * 512:PAD + (tt_ + 1) * 512], pb.v)
                                    fcol = PAD if d == 0 else PAD + SEG - 1
                                    if si > 0:
                                        a0 = LP[:, d, 0, gp, 0:1]
                                        b0 = LP[:, d, 0, gp, 1:2]
                                        nb0 = LP[:, d, 0, gp, 2:3]
                                        f = slice(fcol, fcol + 1)
                                        kb.stt("dve", Are[:, f], car[:, 0:1], a0, Are[:, f], ALU.mult, ALU.add)
                                        kb.stt("dve", Aim[:, f], car[:, 1:2], a0, Aim[:, f], ALU.mult, ALU.add)
                                        kb.stt("dve", Are[:, f], car[:, 1:2], nb0, Are[:, f], ALU.mult, ALU.add)
                                        kb.stt("dve", Aim[:, f], car[:, 0:1], b0, Aim[:, f], ALU.mult, ALU.add)
                                    units.append([gp, j, Are, Aim, Bre_, Bim_, car, xbr, xbi])
                                o_ = slice(PAD, PAD + SEG)
                                for lev in range(NLEV):
                                    dd = 1 << lev
                                    sh = slice(PAD - dd, PAD + SEG - dd) if d == 0 else slice(PAD + dd, PAD + SEG + dd)
                                    for u in units:
                                        gp, j, sr, si_, dr, di = u[0:6]
                                        a = LP[:, d, lev, gp, 0:1]
                                        kb.stt("dve", dr[:, o_], sr[:, sh], a, sr[:, o_], ALU.mult, ALU.add)
                                        kb.stt("dve", di[:, o_], si_[:, sh], a, si_[:, o_], ALU.mult, ALU.add)
                                    for u in units:
                                        gp, j, sr, si_, dr, di = u[0:6]
                                        b = LP[:, d, lev, gp, 1:2]
                                        nb = LP[:, d, lev, gp, 2:3]
                                        kb.stt("dve", dr[:, o_], si_[:, sh], nb, dr[:, o_], ALU.mult, ALU.add)
                                        kb.stt("dve", di[:, o_], sr[:, sh], b, di[:, o_], ALU.mult, ALU.add)
                                        u[2], u[3], u[4], u[5] = dr, di, sr, si_
                                lcol = PAD + SEG - 1 if d == 0 else PAD
                                for u in units:
                                    gp, j, sr, si_, dr, di, car, xbr, xbi = u
                                    kb.copy("pool", car[:, 0:1], sr[:, lcol:lcol + 1])
                                    kb.copy("pool", car[:, 1:2], si_[:, lcol:lcol + 1])
                                    kb.copy("act", xbr.v, sr[:, PAD:PAD + SEG])
                                    kb.copy("act", xbi.v, si_[:, PAD:PAD + SEG])
                                    for tt_ in range(NT5):
                                        kb.mm(pys[tt_].v, CW[:, d, gp, 0, :], xbr[:, tt_ * 512:(tt_ + 1) * 512],
                                              start=(j == 0), stop=False)
                                        kb.mm(pys[tt_].v, CW[:, d, gp, 1, :], xbi[:, tt_ * 512:(tt_ + 1) * 512],
                                              start=False, stop=(j == 3))
                            for tt_ in range(NT5):
                                sl = slice(t0 + tt_ * 512, t0 + (tt_ + 1) * 512)
                                if d == 0:
                                    kb.copy("act", ych[:, sl], pys[tt_].v)
                                else:
                                    kb.tt("dve", ych[:, sl], ych[:, sl], pys[tt_].v, ALU.add)
                    kb.stt("dve", ych[:, :L], uf[:, :L], dT[:, ck:ck + 1], ych[:, :L], ALU.mult, ALU.add)
                    kb.act(go[:, :L], ych[:, :L], AF.Gelu)
                    kb.dma("pool", self.OT[ck, :, off:off + L], go[:, :L])
            kb.barrier()


    def core_dn(self, li):
        kb = self.kb
        Lm = self.Lmax
        NTm = Lm // 128
        with ExitStack() as ph:
            masks = kb.sbuf([128, 4, 128], F32, "dmask", ph)
            kb.dma("sp", masks.v, self.c_masks.rearrange("a p q -> p a q"))
            convw = kb.sbuf([128, 24, 4], F32, "convw", ph)
            kb.dma("sp", convw.v, self.dn_convT[:, :, :])
            onormb = kb.sbuf([128, 128], F32, "onormb", ph)
            kb.dma("sp", onormb.v, self.dn_onorm[0].partition_broadcast(128))
            dtbc = kb.sbuf([16, 1], F32, "dtbc", ph)
            negAc = kb.sbuf([16, 1], F32, "negAc", ph)
            kb.dma("sp", dtbc.v, self.dn_dtb.rearrange("a b -> b a"), allow_slow_non_contiguous=True)
            kb.dma("sp", negAc.v, self.dn_alog.rearrange("a b -> b a"), allow_slow_non_contiguous=True)
            kb.act(negAc.v, negAc.v, AF.Exp)
            kb.ts("dve", negAc.v, negAc.v, -1.0)
            Gt = kb.sbuf([128, NTm, 16], F32, "Gt", ph)
            BETA = kb.sbuf([128, NTm, 16], F32, "BETA", ph)
            NB = kb.sbuf([128, NTm, 16], F32, "NB", ph)
            GCt = kb.sbuf([128, NTm, 16], F32, "GCt", ph)
            CD = kb.sbuf([128, NTm, 16], F32, "CD", ph)
            EGL = kb.sbuf([128, NTm, 16], F32, "EGL", ph)
            BK = kb.sbuf([128, NTm, 16], F32, "BK", ph)
            abT = Ring([kb.sbuf([32, 512], F32, "abT%d" % i, ph) for i in range(1)])
            sgT = Ring([kb.sbuf([32, 512], F32, "sgT%d" % i, ph) for i in range(1)])
            gT = Ring([kb.sbuf([16, 512], F32, "gT%d" % i, ph) for i in range(1)])
            xin = kb.sbuf([128, Lm + 3], F32, "xin", ph)
            qT = kb.sbuf([128, Lm], F32, "dqT", ph)
            kT = kb.sbuf([128, Lm], F32, "dkT", ph)
            zs = kb.sbuf([128, Lm], F32, "dzs", ph)
            vtm = kb.sbuf([128, NTm, 128], F32, "vtm", ph)
            oacc = kb.sbuf([128, NTm, 128], F32, "oacc", ph)
            oT = kb.sbuf([128, Lm], BF16, "doT", ph)
            sqb = Ring([kb.sbuf([128, 512], BF16, "dsq%d" % i, ph) for i in range(2)])
            rnb = Ring([kb.sbuf([128, 512], F32, "drn%d" % i, ph) for i in range(2)])
            Sst = [kb.sbuf([128, 128], F32, "dS%d" % d, ph) for d in range(2)]

            def ring(name, shape, n=2, dt=F32):
                return [Ring([kb.sbuf(shape, dt, "%s%d_%d" % (name, d, i), ph) for i in range(n)]) for d in range(2)]
            rGU, rT1, rDm, rDT, rEG, rG0 = (ring(nm, [128, 128]) for nm in ("GU", "T1", "Dm", "DT", "EG", "G0"))
            rqd, rAT, rwT, rkd, rvn = (ring(nm, [128, 128]) for nm in ("qd", "AT", "wT", "kd", "vn"))
            rP = ring("P", [128, 128], 3)
            rPT = ring("PT", [128, 128], 3)
            rLT = ring("LT", [128, 128], 2)
            rkt = ring("kt", [128, 128], 2)
            rP0 = ring("P0", [128, 128], 2)
            rPT0 = ring("PT0", [128, 128], 2)
            rY = ring("Y", [128, 128], 2)
            lmask = kb.sbuf([128, 2, 7, 128], F32, "lmask", ph)
            kb.dma("sp", lmask.v, self.c_lmask.rearrange("a l p q -> p a l q"))
            rX = ring("X", [128, 256], 3)
            sm = Ring([kb.sbuf([128, 2], F32, "dsm%d" % i, ph) for i in range(3)])
            onr = Ring([kb.sbuf([128, 128], F32, "don%d" % i, ph) for i in range(2)])
            sqj = kb.sbuf([128, 128], F32, "dsqj", ph)
            pss = Ring([kb.psum([128, 512], F32, "dps%d" % i, ph) for i in range(8)])
            ev = [0]

            import os as _os
            _evm = _os.environ.get("DN_EV", "alt")

            def evac(out, ps, same=False):
                if not same:
                    ev[0] += 1
                if _evm == "alt":
                    kb.copy("act" if ev[0] % 3 else "dve", out, ps)
                else:
                    kb.copy(_evm, out, ps)

            for s, L in enumerate(self.seqs):
                off = self.offs[s]
                NT = L // 128
                for t0 in range(0, L, 512):
                    ab, sg, g_ = abT.next(), sgT.next(), gT.next()
                    kb.dma("sp", ab.v, self.ABd[:, off + t0:off + t0 + 512])
                    kb.act(sg.v, ab.v, AF.Sigmoid)
                    kb.act(g_.v, ab[0:16, :], AF.Exp, scale=1.0, bias=dtbc.v)
                    kb.act(g_.v, g_.v, AF.Ln, scale=1.0, bias=self.oneT[0:16, :])
                    kb.ts("dve", g_.v, g_.v, negAc.v)
                    for a in range(4):
                        n = t0 // 128 + a
                        p1 = pss.next()
                        kb.transpose(p1[:, 0:16], g_[:, a * 128:(a + 1) * 128], self.ident[0:16, 0:16])
                        kb.transpose(p1[:, 16:48], sg[:, a * 128:(a + 1) * 128], self.ident[0:32, 0:32])
                        kb.copy("dve", Gt[:, n, :], p1[:, 0:16])
                        kb.copy("dve", BETA[:, n, :], p1[:, 32:48])
                kb.ts("dve", NB[:, :NT, :], BETA[:, :NT, :], -1.0)
                p1 = pss.next()
                kb.mm(p1[:, 0:NT * 8].re("p (n h) -> p n h", h=8), masks[:, 2, :], Gt[:, :NT, 0:8])
                kb.mm(p1[:, 256:256 + NT * 8].re("p (n h) -> p n h", h=8), masks[:, 0, :], Gt[:, :NT, 8:16])
                kb.copy("dve", GCt[:, :NT, 0:8], p1[:, 0:NT * 8].re("p (n h) -> p n h", h=8))
                kb.copy("dve", GCt[:, :NT, 8:16], p1[:, 256:256 + NT * 8].re("p (n h) -> p n h", h=8))
                p2 = pss.next()
                kb.mm(p2[:, 0:NT * 16].re("p (n h) -> p n h", h=16), self.onesf.v, Gt[:, :NT, :])
                p2v = p2[:, 0:NT * 16].re("p (n h) -> p n h", h=16)
                kb.copy("dve", CD[:, :NT, :], p2v)
                kb.tt("dve", EGL[:, :NT, :], CD[:, :NT, :], GCt[:, :NT, :], ALU.subtract)
                kb.act(CD[:, :NT, :], CD[:, :NT, :], AF.Exp)
                kb.act(EGL[:, :NT, :], EGL[:, :NT, :], AF.Exp)
                kb.act(BK[:, :NT, :], GCt[:, :NT, :], AF.Exp)
                kb.tt("dve", BK[:, :NT, :], BK[:, :NT, :], BETA[:, :NT, :], ALU.mult)
                stop = self.cfg.get("dn_stop", 99)
                if stop <= 2:
                    continue

                for hd in range(int(self.cfg.get('dn_heads', 8))):
                    kb.memset("pool", xin[:, 0:1], 0.0)
                    kb.memset("pool", xin[:, L + 1:L + 3], 0.0)
                    for which, chunk, dest in (("q", hd, qT), ("k", 8 + hd, kT), ("v", 16 + hd, zs)):
                        kb.dma("sp", xin[:, 1:L + 1], self.PRE[chunk, :, off:off + L])
                        kb.ts("pool", dest[:, :L], xin[:, 0:L], convw[:, chunk, 0:1])
                        for k_ in range(1, 4):
                            kb.stt("dve", dest[:, :L], xin[:, k_:k_ + L], convw[:, chunk, k_:k_ + 1], dest[:, :L], ALU.mult, ALU.add)
                        kb.act(dest[:, :L], dest[:, :L], AF.Silu)
                        if which != "v":
                            scale = (128.0 ** -0.5) if which == "q" else 1.0
                            for t0 in range(0, L, 512):
                                sq, rn = sqb.next(), rnb.next()
                                kb.act(sq.v, dest[:, t0:t0 + 512], AF.Square)
                                p1 = pss.next()
                                kb.mm(p1.v, self.ones1b.v, sq.v)
                                kb.act(rn.v, p1.v, AF.Sqrt, scale=1.0, bias=self.epsT.v)
                                kb.recip(rn.v, rn.v)
                                kb.stt("dve", dest[:, t0:t0 + 512], dest[:, t0:t0 + 512], scale, rn.v, ALU.mult, ALU.mult)
                    if stop <= 3:
                        continue
                    for n in range(NT):
                        p1 = pss.next()
                        kb.transpose(p1[:, 128:256], zs[:, n * 128:(n + 1) * 128], self.ident.v)
                        evac(vtm[:, n, :], p1[:, 128:256])
                    if stop <= 3.5:
                        continue
                    kb.dma("sp", zs[:, :L], self.PRE[24 + hd, :, off:off + L])
                    kb.act(zs[:, :L], zs[:, :L], AF.Silu)
                    if stop <= 3.7:
                        continue
                    kb.memset("pool", oacc[:, :NT, :], 0.0)
                    for d in range(2):
                        kb.memset("pool", Sst[d], 0.0)
                    if stop <= 4:
                        continue
                    _dirs = [int(x) for x in _os.environ.get("DN_DIR", "01")]
                    def unit(d, n):
                        tl = slice(n * 128, (n + 1) * 128)
                        col = d * 8 + hd
                        g1 = Gt[:, n, col:col + 1]
                        gc = GCt[:, n, col:col + 1]
                        CUM = masks[:, 2 if d == 0 else 0, :]
                        Mstr = masks[:, 1 if d == 0 else 3, :]
                        Msc = masks[:, 2 if d == 0 else 0, :]
                        GU = rGU[d].next()
                        kb.ts("pool", GU.v, CUM, g1)
                        yield
                        pG = pss.next()
                        kb.mm(pG[:, 0:128], self.onesf.v, GU.v)
                        yield
                        G0 = rG0[d].next()
                        kb.copy("dve", G0.v, pG[:, 0:128])
                        yield
                        T1 = rT1[d].next()
                        Dm = rDm[d].next()
                        kb.ts("dve", T1.v, G0.v, gc, 0.0, ALU.subtract, ALU.max)
                        yield
                        kb.act(Dm.v, T1.v, AF.Exp, scale=-1.0)
                        yield
                        kb.stt("dve", Dm.v, Dm.v, NB[:, n, col:col + 1], Mstr, ALU.mult, ALU.mult)
                        yield
                        T2 = rT1[d].next()
                        DT = rDT[d].next()
                        kb.ts("dve", T2.v, G0.v, gc, 0.0, ALU.subtract, ALU.min)
                        yield
                        kb.act(DT.v, T2.v, AF.Exp)
                        yield
                        kb.tt("pool", DT.v, DT.v, Msc, ALU.mult)
                        yield
                        EG = rEG[d].next()
                        kb.act(EG.v, G0.v, AF.Exp)
                        yield
                        qd = rqd[d].next()
                        kb.tt("dve", qd.v, qT[:, tl], EG.v, ALU.mult)
                        yield
                        pK = pss.next()
                        kb.mm(pK[:, 0:128], kT[:, tl], kT[:, tl])
                        yield
                        kb.mm(pK[:, 128:256], kT[:, tl], qT[:, tl])
                        yield
                        P = rP0[d].next()
                        kb.tt("dve", P.v, pK[:, 0:128], Dm.v, ALU.mult)
                        yield
                        AT = rAT[d].next()
                        kb.tt("dve", AT.v, pK[:, 128:256], DT.v, ALU.mult)
                        yield
                        pT_ = pss.next()
                        kb.transpose(pT_[:, 0:128], P.v, self.ident.v)
                        yield
                        PT = rPT0[d].next()
                        evac(PT.v, pT_[:, 0:128])
                        yield
                        pKt = pss.next()
                        kb.transpose(pKt[:, 0:128], kT[:, tl], self.ident.v)
                        yield
                        ktm_t = rkt[d].next()
                        evac(ktm_t.v, pKt[:, 0:128])
                        yield
                        R_ = rX[d].next()
                        kb.ts("pool", R_[:, 0:128], vtm[:, n, :], BETA[:, n, col:col + 1])
                        yield
                        kb.ts("pool", R_[:, 128:256], ktm_t.v, BK[:, n, col:col + 1])
                        yield
                        lm = 0 if d == 0 else 1
                        Tm = rP[d].next()
                        TTm = rPT[d].next()
                        kb.tt("pool", Tm.v, P.v, lmask[:, lm, 0, :], ALU.mult)
                        yield
                        kb.tt("pool", Tm.v, Tm.v, self.ident.v, ALU.add)
                        yield
                        kb.tt("pool", TTm.v, PT.v, lmask[:, 1 - lm, 0, :], ALU.mult)
                        yield
                        kb.tt("pool", TTm.v, TTm.v, self.ident.v, ALU.add)
                        yield
                        for lev in range(1, 7):
                            LT = rLT[d].next()
                            kb.tt("pool", LT.v, PT.v, lmask[:, 1 - lm, lev, :], ALU.mult)
                            yield
                            pY = pss.next()
                            kb.mm(pY[:, 0:128], LT.v, Tm.v)
                            yield
                            Y_ = rY[d].next()
                            evac(Y_.v, pY[:, 0:128])
                            yield
                            pZ = pss.next()
                            kb.mm(pZ[:, 0:128], TTm.v, Y_.v)
                            yield
                            Tn = rP[d].next()
                            kb.tt("dve", Tn.v, pZ[:, 0:128], Tm.v, ALU.add)
                            yield
                            pTT = pss.next()
                            kb.transpose(pTT[:, 0:128], Tn.v, self.ident.v)
                            yield
                            TTn = rPT[d].next()
                            evac(TTn.v, pTT[:, 0:128])
                            yield
                            Tm, TTm = Tn, TTn
                        pX = pss.next()
                        kb.mm(pX[:, 0:256], TTm.v, R_.v)
                        yield
                        X = rX[d].next()
                        evac(X.v, pX[:, 0:256])
                        yield
                        pW = pss.next()
                        kb.transpose(pW[:, 0:128], X[:, 128:256], self.ident.v)
                        yield
                        wT = rwT[d].next()
                        evac(wT.v, pW[:, 0:128])
                        yield
                        kd = rkd[d].next()
                        kb.ts("pool", kd.v, ktm_t.v, EGL[:, n, col:col + 1])
                        yield
                        S_ = Sst[d]
                        p1 = pss.next()
                        kb.mm(p1[:, 0:128], wT.v, S_.v)
                        yield
                        vn = rvn[d].next()
                        kb.tt("dve", vn.v, X[:, 0:128], p1[:, 0:128], ALU.subtract)
                        yield
                        p2 = pss.next()
                        kb.mm(p2[:, 0:128], qd.v, S_.v, start=True, stop=False)
                        yield
                        kb.mm(p2[:, 0:128], AT.v, vn.v, start=False, stop=True)
                        yield
                        kb.tt("dve", oacc[:, n, :], oacc[:, n, :], p2[:, 0:128], ALU.add)
                        yield
                        p3 = pss.next()
                        kb.mm(p3[:, 0:128], kd.v, vn.v)
                        yield
                        kb.stt("dve", S_.v, S_.v, CD[:, n, col:col + 1], p3[:, 0:128], ALU.mult, ALU.add)
                        yield
                    for i in range(NT):
                        alive = [unit(d, i if d == 0 else NT - 1 - i) for d in _dirs]
                        while alive:
                            for g_ in list(alive):
                                try:
                                    next(g_)
                                except StopIteration:
                                    alive.remove(g_)
                    if stop <= 5:
                        continue
                    for n in range(NT):
                        m_ = sm.next()
                        kb.act(sqj.v, oacc[:, n, :], AF.Square, accum_out=m_[:, 0:1])
                        kb.act(m_[:, 1:2], m_[:, 0:1], AF.Sqrt, scale=1.0 / 128.0, bias=self.epsT.v)
                        kb.recip(m_[:, 1:2], m_[:, 1:2])
                        on = onr.next()
                        kb.stt("dve", on.v, oacc[:, n, :], m_[:, 1:2], onormb.v, ALU.mult, ALU.mult)
                        p1 = pss.next()
                        if _os.environ.get("DN_RAW"):
                            kb.transpose(p1[:, 0:128], oacc[:, n, :], self.ident.v)
                            kb.copy("dve", oT[:, n * 128:(n + 1) * 128], p1[:, 0:128])
                            continue
                        kb.transpose(p1[:, 0:128], on.v, self.ident.v)
                        kb.tt("dve", oT[:, n * 128:(n + 1) * 128], p1[:, 0:128], zs[:, n * 128:(n + 1) * 128], ALU.mult)
                    kb.dma("pool", self.OT[hd, :, off:off + L], oT[:, :L])
            kb.barrier()


def _f32(a):
    return np.ascontiguousarray(np.asarray(a, dtype=np.float32))


def host_shared(inp, kinds):
    sh = {}
    sh["ada_w"] = _f32(inp["ada_w"])
    sh["ada_bT"] = _f32(np.asarray(inp["ada_b"]).reshape(4, 48, 128).transpose(0, 2, 1))
    sh["n1g"] = _f32(np.asarray(inp["norm1_g"]).reshape(4, 8, 128).transpose(0, 2, 1))
    sh["n2g"] = _f32(np.asarray(inp["norm2_g"]).reshape(4, 8, 128).transpose(0, 2, 1))
    sh["fing"] = _f32(np.asarray(inp["final_g"]).reshape(8, 128).T)
    for k in ("ffn_w1", "ffn_w3", "ffn_w2"):
        sh[k] = _f32(inp[k])
    sh["c_ident"] = np.eye(128, dtype=np.float32)
    i = np.arange(128)[:, None]
    j = np.arange(128)[None, :]
    sh["c_masks"] = _f32(np.stack([j <= i, j < i, j >= i, j > i]))
    if "da" in kinds:
        sh["da_w_in"] = _f32(inp["da_w_in"])
        sh["da_lam"] = _f32(np.asarray(inp["da_lam"]).reshape(1, 1, 256))
        sh["da_subln_g"] = _f32(np.asarray(inp["da_subln_g"]).reshape(1, 1, 128))
        sh["da_w_out"] = _f32(inp["da_w_out"])
        half = 32
        inv = np.power(np.float32(ROPE_THETA), -np.arange(half, dtype=np.float32) * np.float32(2.0) / np.float32(64))
        ang = np.arange(4096, dtype=np.float32)[None, :] * inv.astype(np.float32)[:, None]
        ang = np.tile(ang.astype(np.float32), (4, 1))
        sh["c_rope"] = _f32(np.stack([np.cos(ang), np.sin(ang)]))
    if "s5" in kinds:
        sh["s5_w_in"] = _f32(inp["s5_w_in"])
        sh["s5_w_glu"] = _f32(inp["s5_w_glu"])
        sh["s5_dT"] = _f32(np.asarray(inp["s5_d"]).reshape(8, 128).T)
        are = np.asarray(inp["s5_a_re"])[0]
        aim = np.asarray(inp["s5_a_im"])[0]
        ldt = np.asarray(inp["s5_log_dt"])[0]
        bre = np.asarray(inp["s5_b_re"])[0]
        bim = np.asarray(inp["s5_b_im"])[0]
        cre = np.asarray(inp["s5_c_re"])[0]
        cim = np.asarray(inp["s5_c_im"])[0]

        def layA(a):
            a = a.reshape(2, 8, 8, 1, 64)
            a = np.broadcast_to(a, (2, 8, 8, 16, 64))
            return a.transpose(0, 2, 3, 1, 4).reshape(2, 128, 8, 64)
        sh["s5_aA"] = _f32(np.stack([layA(are), layA(aim)]))
        d = ldt.reshape(2, 8, 8, 1)
        d = np.broadcast_to(d, (2, 8, 8, 16)).transpose(0, 2, 3, 1).reshape(2, 128, 8)
        sh["s5_dtA"] = _f32(d)

        def layBm(b):
            b = b.reshape(2, 8, 8, 64, 16)
            return b.transpose(0, 2, 4, 1, 3).reshape(2, 128, 8, 64)
        sh["s5_bA"] = _f32(np.stack([layBm(bre), layBm(bim)]))

        def layB(a):
            a = a.reshape(2, 32, 2, 64)
            return a.transpose(0, 2, 3, 1).reshape(2, 128, 32)
        sh["s5_aB"] = _f32(np.stack([layB(are), layB(aim)]))
        d = np.broadcast_to(ldt.reshape(2, 32, 2, 1), (2, 32, 2, 64)).transpose(0, 2, 3, 1).reshape(2, 128, 32)
        sh["s5_dtB"] = _f32(d)

        def layC(c):
            c = c.reshape(2, 32, 2, 16, 64)
            return c.transpose(0, 2, 4, 1, 3).reshape(2, 128, 32, 16)
        sh["s5_cB"] = _f32(np.stack([layC(cre), layC(cim)]))
        sh["c_rowmask"] = _f32((np.arange(128)[:, None] // 16) == np.arange(8)[None, :])
    if "na" in kinds:
        sh["na_w_in"] = _f32(inp["na_w_in"])
        sh["na_w_out"] = _f32(inp["na_w_out"])
        pad = np.zeros((1, 16 * 15 * 31 + 128), np.float32)
        pad[0, 64:64 + 16 * 15 * 31] = np.asarray(inp["na_rpb"], dtype=np.float32).reshape(-1)
        sh["c_antiI"] = _f32(np.eye(64)[::-1])
        sh["na_rpbp"] = pad
        q = np.arange(64)
        cs = np.clip(q - 8, 0, 48)
        kc = np.arange(128) % 64
        valid = (kc[:, None] >= cs[None, :]) & (kc[:, None] < cs[None, :] + 16)
        mm_ = np.broadcast_to(valid[:, None, :], (128, 14, 64))
        sh["c_namask"] = _f32(np.stack([mm_.astype(np.float32), np.where(mm_, 0.0, -30000.0)]))
    if "dn" in kinds:
        sh["dn_w_in"] = _f32(inp["dn_w_in"])
        cw = np.asarray(inp["dn_conv_w"])[0]
        sh["dn_convT"] = _f32(cw.reshape(4, 24, 128).transpose(2, 1, 0))
        sh["dn_alog"] = _f32(np.asarray(inp["dn_a_log"]).reshape(1, 16))
        sh["dn_dtb"] = _f32(np.asarray(inp["dn_dt_bias"]).reshape(1, 16))
        sh["dn_onorm"] = _f32(np.asarray(inp["dn_onorm_g"]).reshape(1, 128))
        sh["dn_w_out"] = _f32(inp["dn_w_out"])
        ci = np.arange(128)[:, None]
        si = np.arange(128)[None, :]
        lm = []
        for lev in range(7):
            h = 1 << lev
            lm.append((ci // (2 * h) == si // (2 * h)) & (ci % (2 * h) >= h) & (si % (2 * h) < h))
        lm = np.stack(lm).astype(np.float32)
        sh["c_lmask"] = _f32(np.stack([lm, lm.transpose(0, 2, 1)]))
    return sh


def host_core(inp, seq_list):
    xs, cs = [], []
    for grp, b in seq_list:
        x = np.asarray(inp["x_prompt" if grp == "p" else "x_sample"][b], dtype=np.float32)
        c = np.asarray(inp["c_prompt" if grp == "p" else "c_sample"][b], dtype=np.float32)
        xs.append(x.T)
        cs.append(c.reshape(8, 128).T)
    xT = np.concatenate(xs, axis=1).reshape(8, 128, -1)
    cT = np.stack(cs, axis=2)
    return {"xT": _f32(xT), "cT": _f32(cT)}


def run_cfg(inp, cfg, core_seq_lists):
    kinds = set(k for _, k in cfg["layers"])
    prog = Prog(cfg)
    sh = host_shared(inp, kinds)
    in_maps = []
    for sl in core_seq_lists:
        m = dict(sh)
        m.update(host_core(inp, sl))
        m = {k: v for k, v in m.items() if k in prog.din}
        in_maps.append(m)
    import os as _os
    if _os.environ.get("KTRACE"):
        res = run_bass_kernel_spmd(prog.nc, in_maps, core_ids=list(range(len(core_seq_lists))), trace=True)
        print("EXEC_TIME_NS", res.exec_time_ns, flush=True)
    else:
        res = run_bass_kernel_spmd(prog.nc, in_maps, core_ids=list(range(len(core_seq_lists))))
    outs = []
    prog.raw = res.results
    for ci, sl in enumerate(core_seq_lists):
        yT = np.asarray(res.results[ci]["yT"]).reshape(1024, -1)
        o = []
        off = 0
        for L in cfg["seqs"]:
            o.append(np.ascontiguousarray(yT[:, off:off + L].T))
            off += L
        outs.append(o)
    return outs, prog


FULL_LAYERS = [(0, "da"), (1, "s5"), (2, "na"), (3, "dn")]


def kernel(**inputs):
    B = 16
    cfg = {"seqs": [2048, 2048, 4096, 4096], "layers": FULL_LAYERS}
    lists = [[("p", 2 * c), ("p", 2 * c + 1), ("s", 2 * c), ("s", 2 * c + 1)] for c in range(8)]
    outs, _ = run_cfg(inputs, cfg, lists)
    yp = np.zeros((B, 2048, 1024), np.float32)
    ys = np.zeros((B, 4096, 1024), np.float32)
    for c in range(8):
        yp[2 * c], yp[2 * c + 1], ys[2 * c], ys[2 * c + 1] = outs[c]
    return (yp, ys)
```

```python
import math
from contextlib import ExitStack

import numpy as np
import concourse.bass as bass
import concourse.mybir as mybir
from concourse.bass_utils import run_bass_kernel_spmd

F32 = mybir.dt.float32
BF16 = mybir.dt.bfloat16
AF = mybir.ActivationFunctionType
ALU = mybir.AluOpType
AX = mybir.AxisListType

RANGE = 16000
NDSEM = 6
SAME_ENGINE_SYNC = True


class V:
    __slots__ = ("t", "ap")

    def __init__(self, t, ap):
        self.t = t
        self.ap = ap

    def __getitem__(self, idx):
        return V(self.t, self.ap[idx])

    def re(self, s, **kw):
        return V(self.t, self.ap.rearrange(s, **kw))

    def bc(self, shape):
        return V(self.t, self.ap.to_broadcast(list(shape)))


class T:
    __slots__ = ("h", "w", "r", "name")

    def __init__(self, h, name=""):
        self.h = h
        self.w = {}
        self.r = {}
        self.name = name

    def __getitem__(self, idx):
        return V(self, self.h[idx])

    @property
    def v(self):
        return V(self, self.h[:])


class KB:
    def __init__(self, nc, stack):
        self.nc = nc
        self.stack = stack
        self.engs = {"pe": nc.tensor, "act": nc.scalar, "dve": nc.vector,
                     "pool": nc.gpsimd, "sp": nc.sync}
        self.cnt = {e: 0 for e in self.engs}
        self.csem = {}
        self.seen = {e: {} for e in self.engs}
        self.dq = {}
        self.dnext = {e: 0 for e in self.engs}
        self.out_tokens = []
        self.ninst = 0
        self.nwait = 0
        self.uid = 0

    def sbuf(self, shape, dt, name, stack=None):
        self.uid += 1
        st = stack if stack is not None else self.stack
        t = st.enter_context(self.nc.sbuf_tensor("%s_%d" % (name, self.uid), list(shape), dt))
        return T(t, name)

    def psum(self, shape, dt, name, stack=None):
        self.uid += 1
        st = stack if stack is not None else self.stack
        t = st.enter_context(self.nc.psum_tensor("%s_%d" % (name, self.uid), list(shape), dt))
        return T(t, name)

    def _sem(self, sk):
        if sk not in self.csem:
            self.csem[sk] = self.stack.enter_context(
                self.nc.semaphore("s_" + "_".join(str(x) for x in sk)))
        return self.csem[sk]

    def _wait(self, eng, sk, v):
        if sk[0] == "c" and sk[1] == eng and (eng == "pe" or not SAME_ENGINE_SYNC):
            return
        if self.seen[eng].get(sk, 0) >= v:
            return
        self.engs[eng].wait_ge(self._sem(sk), v)
        self.seen[eng][sk] = v
        self.nwait += 1

    def _deps(self, eng, reads, writes):
        toks = {}
        for b in reads:
            for sk, v in b.w.items():
                if toks.get(sk, 0) < v:
                    toks[sk] = v
        for b in writes:
            for sk, v in b.w.items():
                if toks.get(sk, 0) < v:
                    toks[sk] = v
            for sk, v in b.r.items():
                if toks.get(sk, 0) < v:
                    toks[sk] = v
        for sk, v in toks.items():
            self._wait(eng, sk, v)

    def _commit(self, sk, v, reads, writes):
        for b in reads:
            if b.r.get(sk, 0) < v:
                b.r[sk] = v
        for b in writes:
            b.w[sk] = v
            b.r = {}

    def _split(self, kwargs, outkeys):
        reads, writes = [], []
        kw = {}
        for k, a in kwargs.items():
            if isinstance(a, V):
                (writes if k in outkeys else reads).append(a.t)
                kw[k] = a.ap
            elif isinstance(a, T):
                (writes if k in outkeys else reads).append(a)
                kw[k] = a.h[:]
            else:
                kw[k] = a
        return kw, reads, writes

    def I(self, eng, meth, _extra_reads=(), _extra_writes=(), **kwargs):
        kw, reads, writes = self._split(kwargs, ("out", "accum_out"))
        reads = list(reads) + list(_extra_reads)
        writes = list(writes) + list(_extra_writes)
        self._deps(eng, reads, writes)
        ins = getattr(self.engs[eng], meth)(**kw)
        idx = self.cnt[eng]
        self.cnt[eng] += 1
        sk = ("c", eng, idx // RANGE)
        v = idx % RANGE + 1
        ins.then_inc(self._sem(sk), 1)
        self._commit(sk, v, reads, writes)
        self.ninst += 1

    def dma(self, q, out, in_, is_output=False, **kw):
        reads, writes = [], []
        if isinstance(out, (V, T)):
            writes.append(out.t if isinstance(out, V) else out)
            out = out.ap if isinstance(out, V) else out.h[:]
        if isinstance(in_, (V, T)):
            reads.append(in_.t if isinstance(in_, V) else in_)
            in_ = in_.ap if isinstance(in_, V) else in_.h[:]
        self._deps(q, reads, writes)
        lst = self.dq.setdefault(q, [])
        if len(lst) < NDSEM:
            lst.append([("d", q, len(lst), 0), 0])
            i = len(lst) - 1
        else:
            i = self.dnext[q] % NDSEM
        self.dnext[q] += 1
        ent = lst[i]
        if (ent[1] + 1) * 16 > RANGE:
            self._wait(q, ent[0], ent[1] * 16)
            ent[0] = ("d", q, i, ent[0][3] + 1)
            ent[1] = 0
        if ent[1] > 0:
            self._wait(q, ent[0], ent[1] * 16)
        ent[1] += 1
        ins = self.engs[q].dma_start(out=out, in_=in_, **kw)
        ins.then_inc(self._sem(ent[0]), 16)
        self._commit(ent[0], ent[1] * 16, reads, writes)
        if is_output:
            self.out_tokens.append((ent[0], ent[1] * 16))
        self.ninst += 1

    def _latest(self):
        toks = []
        for e in self.engs:
            if self.cnt[e] > 0:
                idx = self.cnt[e] - 1
                toks.append((("c", e, idx // RANGE), idx % RANGE + 1))
        for q, lst in self.dq.items():
            for ent in lst:
                if ent[1] > 0:
                    toks.append((ent[0], ent[1] * 16))
        return toks

    def barrier(self):
        toks = self._latest()
        for e in self.engs:
            for sk, v in toks:
                if sk[0] == "c" and sk[1] == e:
                    continue
                self._wait(e, sk, v)

    def finish(self, eng="sp"):
        for sk, v in self.out_tokens:
            self._wait(eng, sk, v)
        for sk, v in self._latest():
            self._wait(eng, sk, v)

    def mm(self, out, lhsT, rhs, start=True, stop=True, nocheck=False):
        if getattr(self, "f32r", False) and lhsT.ap.dtype == F32 and rhs.ap.dtype == F32:
            lhsT = V(lhsT.t, lhsT.ap.bitcast(mybir.dt.float32r))
            rhs = V(rhs.t, rhs.ap.bitcast(mybir.dt.float32r))
        if nocheck:
            self.I("pe", "matmul", out=out, lhsT=lhsT, rhs=rhs, start=start, stop=stop, skip_group_check=True)
        else:
            self.I("pe", "matmul", out=out, lhsT=lhsT, rhs=rhs, start=start, stop=stop)

    def transpose(self, out, in_, ident):
        self.I("pe", "transpose", out=out, in_=in_, identity=ident)

    def act(self, out, in_, func, scale=1.0, bias=None, accum_out=None):
        kw = dict(out=out, in_=in_, func=func, scale=scale)
        if bias is not None:
            kw["bias"] = bias
        if accum_out is not None:
            kw["accum_out"] = accum_out
        self.I("act", "activation", **kw)

    def tt(self, eng, out, in0, in1, op):
        self.I(eng, "tensor_tensor", out=out, in0=in0, in1=in1, op=op)

    def ts(self, eng, out, in0, s1, s2=None, op0=ALU.mult, op1=None):
        if s2 is None:
            self.I(eng, "tensor_scalar", out=out, in0=in0, scalar1=s1, scalar2=None, op0=op0)
        else:
            self.I(eng, "tensor_scalar", out=out, in0=in0, scalar1=s1, scalar2=s2, op0=op0, op1=op1)

    def stt(self, eng, out, in0, scalar, in1, op0, op1):
        self.I(eng, "scalar_tensor_tensor", out=out, in0=in0, scalar=scalar, in1=in1, op0=op0, op1=op1)

    def copy(self, eng, out, in_):
        if eng == "act":
            self.I("act", "activation", out=out, in_=in_, func=AF.Copy)
        else:
            self.I(eng, "tensor_copy", out=out, in_=in_)

    def recip(self, out, in_):
        self.I("dve", "reciprocal", out=out, in_=in_)

    def memset(self, eng, out, val):
        t = out.t if isinstance(out, V) else out
        ap = out.ap if isinstance(out, V) else out.h[:]
        self.I(eng, "memset", ap=ap, constant=val, _extra_writes=[t])


def sub(t, idx, name=""):
    return T(t.h[idx], name or t.name)


class Ring:
    def __init__(self, items):
        self.items = items
        self.i = 0

    def next(self):
        x = self.items[self.i % len(self.items)]
        self.i += 1
        return x


D = 1024
FH = 2816
NFC = FH // 128
EPS = 1e-6
ROPE_THETA = 10000.0


class Prog:
    def __init__(self, cfg):
        self.cfg = cfg
        self.seqs = cfg["seqs"]
        self.NS = len(self.seqs)
        self.offs = [sum(self.seqs[:i]) for i in range(self.NS)]
        self.Ltot = sum(self.seqs)
        self.Lmax = max(self.seqs)
        self.layers = cfg["layers"]
        self.nc = bass.Bass("TRN2", target_bir_lowering=False)
        self.din = {}
        self.build()

    def inp(self, name, shape, dt=F32):
        t = self.nc.dram_tensor(name, list(shape), dt, kind="ExternalInput")
        self.din[name] = t
        return t.ap()

    def scratch(self, name, shape, dt):
        if self.cfg.get("dbg"):
            return self.nc.dram_tensor(name, list(shape), dt, kind="ExternalOutput").ap()
        return self.nc.dram_tensor(name, list(shape), dt).ap()

    def build(self):
        nc = self.nc
        NS, Ltot = self.NS, self.Ltot
        kinds = set(k for _, k in self.layers)
        self.xT = self.inp("xT", [8, 128, Ltot])
        self.cT = self.inp("cT", [128, 8, NS])
        self.ada_w = self.inp("ada_w", [4, D, 6 * D])
        self.ada_bT = self.inp("ada_bT", [4, 128, 48])
        self.n1g = self.inp("n1g", [4, 128, 8])
        self.n2g = self.inp("n2g", [4, 128, 8])
        self.fing = self.inp("fing", [128, 8])
        self.ffn_w1 = self.inp("ffn_w1", [4, D, FH])
        self.ffn_w3 = self.inp("ffn_w3", [4, D, FH])
        self.ffn_w2 = self.inp("ffn_w2", [4, FH, D])
        self.c_ident = self.inp("c_ident", [128, 128])
        self.c_masks = self.inp("c_masks", [4, 128, 128])
        if "da" in kinds:
            self.da_w_in = self.inp("da_w_in", [1, D, 3 * D])
            self.da_lam = self.inp("da_lam", [1, 1, 256])
            self.da_subln_g = self.inp("da_subln_g", [1, 1, 128])
            self.da_w_out = self.inp("da_w_out", [1, D, D])
            self.c_rope = self.inp("c_rope", [2, 128, 4096])
        if "s5" in kinds:
            self.s5_w_in = self.inp("s5_w_in", [1, D, D])
            self.s5_w_glu = self.inp("s5_w_glu", [1, D, 2 * D])
            self.s5_dT = self.inp("s5_dT", [128, 8])
            self.s5_aA = self.inp("s5_aA", [2, 2, 128, 8, 64])
            self.s5_dtA = self.inp("s5_dtA", [2, 128, 8])
            self.s5_bA = self.inp("s5_bA", [2, 2, 128, 8, 64])
            self.s5_aB = self.inp("s5_aB", [2, 2, 128, 32])
            self.s5_dtB = self.inp("s5_dtB", [2, 128, 32])
            self.s5_cB = self.inp("s5_cB", [2, 2, 128, 32, 16])
            self.c_rowmask = self.inp("c_rowmask", [128, 8])
        if "na" in kinds:
            self.na_w_in = self.inp("na_w_in", [1, D, 3 * D])
            self.na_w_out = self.inp("na_w_out", [1, D, D])
            self.na_rpbp = self.inp("na_rpbp", [1, 16 * 15 * 31 + 128])
            self.c_namask = self.inp("c_namask", [2, 128, 14, 64])
            self.c_antiI = self.inp("c_antiI", [64, 64])
        if "dn" in kinds:
            self.dn_w_in = self.inp("dn_w_in", [1, D, 4 * D + 32])
            self.dn_convT = self.inp("dn_convT", [128, 24, 4])
            self.dn_alog = self.inp("dn_alog", [1, 16])
            self.dn_dtb = self.inp("dn_dtb", [1, 16])
            self.dn_onorm = self.inp("dn_onorm", [1, 128])
            self.dn_w_out = self.inp("dn_w_out", [1, D, D])
            self.c_lmask = self.inp("c_lmask", [2, 7, 128, 128])
        self.yT = nc.dram_tensor("yT", [8, 128, Ltot], F32, kind="ExternalOutput").ap()
        self.XR = self.scratch("XR", [8, 128, Ltot], F32)
        self.OT = self.scratch("OT", [8, 128, Ltot], BF16)
        if kinds & {"da", "na"}:
            self.QT = self.scratch("QT", [8, 128, Ltot], BF16)
            self.KT = self.scratch("KT", [8, 128, Ltot], BF16)
            self.Vd = self.scratch("Vd", [Ltot, D], BF16)
        if "s5" in kinds:
            self.UT = self.scratch("UT", [8, 128, Ltot], F32)
        if "dn" in kinds:
            self.PRE = self.scratch("PRE", [32, 128, Ltot], F32)
            self.ABd = self.scratch("ABd", [32, Ltot], F32)

        with ExitStack() as st:
            kb = KB(nc, st)
            self.kb = kb
            self.ident = kb.sbuf([128, 128], F32, "ident")
            self.identb = kb.sbuf([128, 128], BF16, "identb")
            self.onesb = kb.sbuf([128, 128], BF16, "onesb")
            self.ones1b = kb.sbuf([128, 128], BF16, "ones1b")
            self.onesf = kb.sbuf([128, 128], F32, "onesf")
            self.epsT = kb.sbuf([128, 1], F32, "epsT")
            self.oneT = kb.sbuf([128, 1], F32, "oneT")
            self.sct = kb.sbuf([128, 8, NS], F32, "sct")
            self.modt = kb.sbuf([128, 48, NS], F32, "modt")
            self.A1 = kb.sbuf([128, 8, NS], F32, "A1")
            self.A2 = kb.sbuf([128, 8, NS], F32, "A2")
            self.gn = kb.sbuf([128, 3, 8], F32, "gn")
            kb.dma("sp", self.ident.v, self.c_ident[:, :])
            kb.dma("pool", self.identb.v, self.c_ident[:, :])
            kb.memset("dve", self.onesb, 1.0 / 1024.0)
            kb.memset("dve", self.ones1b, 1.0)
            kb.memset("dve", self.onesf, 1.0)
            kb.memset("dve", self.epsT, EPS)
            kb.memset("dve", self.oneT, 1.0)
            kb.dma("sp", self.sct.v, self.cT[:, :, :])
            kb.act(self.sct.v, self.sct.v, AF.Silu)
            kb.dma("sp", self.gn[:, 2, :], self.fing[:, :])
            first = True
            for li, kind in self.layers:
                xsrc = self.xT if first else self.XR
                first = False
                self.phase_mod(li)
                if kind != "none":
                    self.phase_pre(li, kind, xsrc)
                    getattr(self, "core_" + kind)(li)
                self.phase_post(li, kind, xsrc)
            self.phase_final(self.xT if first else self.XR)
            kb.finish()
        self.stats = (kb.ninst, kb.nwait)

    def phase_mod(self, li):
        kb, NS = self.kb, self.NS
        with ExitStack() as ph:
            wts = Ring([kb.sbuf([128, 6 * D], F32, "adaw%d" % i, ph) for i in range(2)])
            acc = kb.sbuf([128, 48, NS], F32, "modacc", ph)
            bt = kb.sbuf([128, 48], F32, "adab", ph)
            pss = Ring([kb.psum([128, 48, NS], F32, "psmod%d" % i, ph) for i in range(2)])
            kb.dma("sp", bt.v, self.ada_bT[li])
            kb.dma("sp", self.gn[:, 0, :], self.n1g[li])
            kb.dma("sp", self.gn[:, 1, :], self.n2g[li])
            for k in range(8):
                wt = wts.next()
                kb.dma("sp", wt.v, self.ada_w[li, k * 128:(k + 1) * 128, :])
                ps = pss.next()
                for n in range(48):
                    kb.mm(ps[:, n, :], wt[:, n * 128:(n + 1) * 128], self.sct[:, k, :])
                if k == 0:
                    kb.copy("dve", acc.v, ps.v)
                else:
                    kb.tt("dve", acc.v, acc.v, ps.v, ALU.add)
            for s in range(NS):
                kb.tt("dve", self.modt[:, :, s], acc[:, :, s], bt.v, ALU.add)
                kb.stt("dve", self.A1[:, :, s], self.modt[:, 8:16, s], 1.0, self.gn[:, 0, :], ALU.add, ALU.mult)
                kb.stt("dve", self.A2[:, :, s], self.modt[:, 32:40, s], 1.0, self.gn[:, 1, :], ALU.add, ALU.mult)
            kb.barrier()

    def norm_mod(self, xt, N, A, B, ht, sq, rs, psum_ring, out_f32=False):
        kb = self.kb
        kb.act(sq[:, :, :N], xt[:, :, :N], AF.Square)
        ps = psum_ring.next()
        for c in range(8):
            kb.mm(ps[:, :N], self.onesb.v, sq[:, c, :N], start=(c == 0), stop=(c == 7))
        kb.act(rs[:, :N], ps[:, :N], AF.Sqrt, scale=1.0, bias=self.epsT.v)
        kb.recip(rs[:, :N], rs[:, :N])
        for c in range(8):
            if B is None:
                kb.stt("dve", ht[:, c, :N], xt[:, c, :N], A[:, c:c + 1], rs[:, :N], ALU.mult, ALU.mult)
            else:
                kb.stt("dve", sq[:, c, :N], xt[:, c, :N], A[:, c:c + 1], rs[:, :N], ALU.mult, ALU.mult)
                kb.act(ht[:, c, :N], sq[:, c, :N], AF.Identity, scale=1.0, bias=B[:, c:c + 1])

    def tiles(self, N):
        for s, L in enumerate(self.seqs):
            for t0 in range(0, L, N):
                yield s, self.offs[s] + t0, t0, min(N, L - t0)

    def phase_final(self, xsrc):
        kb = self.kb
        N = 512
        with ExitStack() as ph:
            xts = Ring([kb.sbuf([128, 8, N], F32, "fx%d" % i, ph) for i in range(2)])
            yts = Ring([kb.sbuf([128, 8, N], F32, "fy%d" % i, ph) for i in range(2)])
            sq = kb.sbuf([128, 8, N], BF16, "fsq", ph)
            rs = kb.sbuf([128, N], F32, "frs", ph)
            pss = Ring([kb.psum([128, 512], F32, "fps%d" % i, ph) for i in range(2)])
            for s, g0, t0, n in self.tiles(N):
                xt = xts.next()
                yt = yts.next()
                kb.dma("sp", xt[:, :, :n], xsrc[:, :, g0:g0 + n].rearrange("c p t -> p c t"))
                self.norm_mod(xt, n, self.gn[:, 2, :], None, yt, sq, rs, pss)
                kb.dma("pool", self.yT[:, :, g0:g0 + n].rearrange("c p t -> p c t"), yt[:, :, :n], is_output=True)
            kb.barrier()

    def phase_post(self, li, kind, xsrc):
        kb = self.kb
        N = 256
        with ExitStack() as ph:
            w1 = kb.sbuf([128, 8, FH], BF16, "w1", ph)
            w3 = kb.sbuf([128, 8, FH], BF16, "w3", ph)
            w2 = kb.sbuf([128, NFC, D], BF16, "w2", ph)
            kb.dma("pool", w1.v, self.ffn_w1[li].rearrange("(k p) n -> p k n", p=128))
            kb.dma("pool", w3.v, self.ffn_w3[li].rearrange("(k p) n -> p k n", p=128))
            kb.dma("pool", w2.v, self.ffn_w2[li].rearrange("(k p) n -> p k n", p=128))
            wo = None
            if kind in ("da", "na", "dn"):
                wsrc = {"da": self.da_w_out, "na": self.na_w_out, "dn": self.dn_w_out}[kind] if False else getattr(self, kind + "_w_out")
                wo = kb.sbuf([128, 8, D], BF16, "wo", ph)
                kb.dma("pool", wo.v, wsrc[0].rearrange("(k p) n -> p k n", p=128))
            elif kind == "s5":
                wo = kb.sbuf([128, 8, 2 * D], BF16, "wo", ph)
                kb.dma("pool", wo.v, self.s5_w_glu[0].rearrange("(k p) n -> p k n", p=128))
            nb = 1 if kind == "s5" else 2
            xts = Ring([kb.sbuf([128, 8, N], F32, "px%d" % i, ph) for i in range(nb)])
            ots = Ring([kb.sbuf([128, 8, N], BF16, "po%d" % i, ph) for i in range(nb)]) if wo is not None else None
            ht = kb.sbuf([128, 8, N], BF16, "ph", ph)
            sq = kb.sbuf([128, 8, N], BF16, "psq", ph)
            gt = kb.sbuf([128, NFC, N], BF16, "pg", ph)
            rs = kb.sbuf([128, N], F32, "prs", ph)
            tmps = Ring([kb.sbuf([128, N], F32, "ptmp%d" % i, ph) for i in range(3)])
            pss = Ring([kb.psum([128, 512], F32, "pps%d" % i, ph) for i in range(8)])
            for s, g0, t0, n in self.tiles(N):
                xt = xts.next()
                kb.dma("sp", xt[:, :, :n], xsrc[:, :, g0:g0 + n].rearrange("c p t -> p c t"))
                G1 = self.modt[:, 16:24, s]
                G2 = self.modt[:, 40:48, s]
                if wo is not None:
                    ot = ots.next()
                    kb.dma("sp", ot[:, :, :n], self.OT[:, :, g0:g0 + n].rearrange("c p t -> p c t"))
                    for m in range(8):
                        ps = pss.next()
                        for k in range(8):
                            kb.mm(ps[:, :n], wo[:, k, m * 128:(m + 1) * 128], ot[:, k, :n], start=(k == 0), stop=(k == 7))
                        if kind == "s5":
                            ps2 = pss.next()
                            for k in range(8):
                                kb.mm(ps2[:, :n], wo[:, k, D + m * 128:D + (m + 1) * 128], ot[:, k, :n], start=(k == 0), stop=(k == 7))
                            tg = tmps.next()
                            kb.act(tg[:, :n], ps2[:, :n], AF.Sigmoid)
                            kb.tt("dve", tg[:, :n], ps[:, :n], tg[:, :n], ALU.mult)
                            kb.stt("dve", xt[:, m, :n], tg[:, :n], G1[:, m:m + 1], xt[:, m, :n], ALU.mult, ALU.add)
                        else:
                            kb.stt("dve", xt[:, m, :n], ps[:, :n], G1[:, m:m + 1], xt[:, m, :n], ALU.mult, ALU.add)
                self.norm_mod(xt, n, self.A2[:, :, s], self.modt[:, 24:32, s], ht, sq, rs, pss)
                for f in range(NFC):
                    pa = pss.next()
                    pb = pss.next()
                    for k in range(8):
                        kb.mm(pa[:, :n], w1[:, k, f * 128:(f + 1) * 128], ht[:, k, :n], start=(k == 0), stop=(k == 7))
                    for k in range(8):
                        kb.mm(pb[:, :n], w3[:, k, f * 128:(f + 1) * 128], ht[:, k, :n], start=(k == 0), stop=(k == 7))
                    tg = tmps.next()
                    kb.act(tg[:, :n], pa[:, :n], AF.Silu)
                    kb.tt("dve", gt[:, f, :n], tg[:, :n], pb[:, :n], ALU.mult)
                for m in range(8):
                    ps = pss.next()
                    for f in range(NFC):
                        kb.mm(ps[:, :n], w2[:, f, m * 128:(m + 1) * 128], gt[:, f, :n], start=(f == 0), stop=(f == NFC - 1))
                    kb.stt("dve", xt[:, m, :n], ps[:, :n], G2[:, m:m + 1], xt[:, m, :n], ALU.mult, ALU.add)
                kb.dma("pool", self.XR[:, :, g0:g0 + n].rearrange("c p t -> p c t"), xt[:, :, :n])
            kb.barrier()


    def phase_pre(self, li, kind, xsrc):
        kb = self.kb
        N = 512
        with ExitStack() as ph:
            xts = Ring([kb.sbuf([128, 8, N], F32, "qx%d" % i, ph) for i in range(2)])
            ht = kb.sbuf([128, 8, N], BF16, "qh", ph)
            sq = kb.sbuf([128, 8, N], BF16, "qsq", ph)
            rs = kb.sbuf([128, N], F32, "qrs", ph)
            pss = Ring([kb.psum([128, 512], F32, "qps%d" % i, ph) for i in range(8)])
            if kind in ("da", "na"):
                wsrc = self.da_w_in if kind == "da" else self.na_w_in
                wq = kb.sbuf([128, 8, D], BF16, "wq", ph)
                wk = kb.sbuf([128, 8, D], BF16, "wk", ph)
                wv = kb.sbuf([128, 8, D], BF16, "wv", ph)
                w3d = wsrc[0].rearrange("(k p) n -> p k n", p=128)
                kb.dma("pool", wq.v, w3d[:, :, 0:D])
                kb.dma("pool", wk.v, w3d[:, :, D:2 * D])
                kb.dma("pool", wv.v, w3d[:, :, 2 * D:3 * D])
                if kind == "da":
                    wq2 = kb.sbuf([128, 8, 16, 2, 32], BF16, "wq2", ph)
                    wk2 = kb.sbuf([128, 8, 16, 2, 32], BF16, "wk2", ph)
                    for w2_, c0 in ((wq2, 0), (wk2, D)):
                        src = w3d[:, :, c0:c0 + D].rearrange("p k (h two d) -> p k h two d", two=2, d=32)
                        for k in range(8):
                            kb.dma("pool", w2_[:, k, :, 0, :], src[:, k, :, 1, :])
                            kb.dma("pool", w2_[:, k, :, 1, :], src[:, k, :, 0, :])
                        kb.ts("dve", w2_[:, :, :, 0, :], w2_[:, :, :, 0, :], -1.0)
                    cst = Ring([kb.sbuf([128, 2, N], F32, "cs%d" % i, ph) for i in range(2)])
                    t1s = Ring([kb.sbuf([128, N], F32, "rt1%d" % i, ph) for i in range(2)])
                    t2s = Ring([kb.sbuf([128, N], F32, "rt2%d" % i, ph) for i in range(2)])
                qo = Ring([kb.sbuf([128, 8, N], BF16, "qo%d" % i, ph) for i in range(2)])
                vo = Ring([kb.sbuf([128, 4, D], BF16, "vo%d" % i, ph) for i in range(2)])
            elif kind == "s5":
                wu = kb.sbuf([128, 8, D], BF16, "wu", ph)
                kb.dma("pool", wu.v, self.s5_w_in[0].rearrange("(k p) n -> p k n", p=128))
                uo = Ring([kb.sbuf([128, 8, N], F32, "uo%d" % i, ph) for i in range(2)])
            elif kind == "dn":
                wd = kb.sbuf([128, 8, 4 * D + 32], BF16, "wd", ph)
                kb.dma("pool", wd.v, self.dn_w_in[0].rearrange("(k p) n -> p k n", p=128))
                po = Ring([kb.sbuf([128, 8, N], F32, "po%d" % i, ph) for i in range(2)])
                abo = Ring([kb.sbuf([32, N], F32, "abo%d" % i, ph) for i in range(2)])
            ev = [0]

            def evac(out, ps):
                ev[0] += 1
                kb.copy("act" if ev[0] % 2 else "dve", out, ps)

            def proj(w, col0, n):
                ps = pss.next()
                for k in range(8):
                    kb.mm(ps[:, :n], w[:, k, col0:col0 + 128], ht[:, k, :n], start=(k == 0), stop=(k == 7))
                return ps

            for s, g0, t0, n in self.tiles(N):
                xt = xts.next()
                kb.dma("sp", xt[:, :, :n], xsrc[:, :, g0:g0 + n].rearrange("c p t -> p c t"))
                self.norm_mod(xt, n, self.A1[:, :, s], self.modt[:, 0:8, s], ht, sq, rs, pss)
                if kind in ("da", "na"):
                    if kind == "da":
                        cs = cst.next()
                        kb.dma("sp", cs[:, :, :n], self.c_rope[:, :, t0:t0 + n].rearrange("a p t -> p a t"))
                    for w, w2_, dst in ((wq, wq2 if kind == "da" else None, self.QT), (wk, wk2 if kind == "da" else None, self.KT)):
                        o = qo.next()
                        for m in range(8):
                            ps = proj(w, m * 128, n)
                            if kind == "da":
                                w2v = V(w2_, w2_.h[:].rearrange("p k h two d -> p k (h two d)"))
                                ps2 = pss.next()
                                for k in range(8):
                                    kb.mm(ps2[:, :n], w2v[:, k, m * 128:(m + 1) * 128], ht[:, k, :n], start=(k == 0), stop=(k == 7))
                                t1 = t1s.next()
                                t2 = t2s.next()
                                kb.tt("dve", t1[:, :n], ps[:, :n], cs[:, 0, :n], ALU.mult)
                                kb.tt("dve", t2[:, :n], ps2[:, :n], cs[:, 1, :n], ALU.mult)
                                kb.tt("pool", o[:, m, :n], t1[:, :n], t2[:, :n], ALU.add)
                            else:
                                evac(o[:, m, :n], ps[:, :n])
                        kb.dma("pool", dst[:, :, g0:g0 + n].rearrange("c p t -> p c t"), o[:, :, :n])
                    v = vo.next()
                    for ts_ in range(n // 128):
                        for hf in range(2):
                            ps = pss.next()
                            for k in range(8):
                                kb.mm(ps.v, ht[:, k, ts_ * 128:(ts_ + 1) * 128], wv[:, k, hf * 512:(hf + 1) * 512], start=(k == 0), stop=(k == 7))
                            evac(v[:, ts_, hf * 512:(hf + 1) * 512], ps.v)
                    kb.dma("pool", self.Vd[g0:g0 + n, :].rearrange("(a p) e -> p a e", p=128), v[:, :n // 128, :])
                elif kind == "s5":
                    o = uo.next()
                    for m in range(8):
                        ps = proj(wu, m * 128, n)
                        evac(o[:, m, :n], ps[:, :n])
                    kb.dma("pool", self.UT[:, :, g0:g0 + n].rearrange("c p t -> p c t"), o[:, :, :n])
                elif kind == "dn":
                    for grp in range(4):
                        o = po.next()
                        for m in range(8):
                            ps = proj(wd, (grp * 8 + m) * 128, n)
                            evac(o[:, m, :n], ps[:, :n])
                        kb.dma("pool", self.PRE[grp * 8:(grp + 1) * 8, :, g0:g0 + n].rearrange("c p t -> p c t"), o[:, :, :n])
                    ps = pss.next()
                    for k in range(8):
                        kb.mm(ps[0:32, :n], wd[:, k, 4 * D:4 * D + 32], ht[:, k, :n], start=(k == 0), stop=(k == 7))
                    ab = abo.next()
                    evac(ab[:, :n], ps[0:32, :n])
                    kb.dma("pool", self.ABd[:, g0:g0 + n], ab[:, :n])
            kb.barrier()


    def core_da(self, li):
        kb = self.kb
        lambda_init = 0.8 - 0.6 * math.exp(-0.3 * li)
        Lm = self.Lmax
        NTm = Lm // 128
        with ExitStack() as ph:
            lamt = kb.sbuf([128, 4, 64], F32, "lamt", ph)
            lsum = kb.sbuf([128, 2], F32, "lsum", ph)
            lscr = kb.sbuf([128, 64], F32, "lscr", ph)
            neglam = kb.sbuf([128, 1], F32, "neglam", ph)
            gsub = kb.sbuf([128, 128], F32, "gsub", ph)
            kb.dma("sp", lamt.v.re("p a b -> p (a b)"), self.da_lam[0, 0].partition_broadcast(128))
            kb.dma("sp", gsub.v, self.da_subln_g[0, 0].partition_broadcast(128))
            for j in range(2):
                kb.tt("dve", lscr.v, lamt[:, 2 * j, :], lamt[:, 2 * j + 1, :], ALU.mult)
                kb.I("dve", "reduce_sum", out=lsum[:, j:j + 1], in_=lscr.v, axis=AX.X)
            kb.act(lsum.v, lsum.v, AF.Exp)
            kb.tt("dve", neglam.v, lsum[:, 1:2], lsum[:, 0:1], ALU.subtract)
            kb.ts("dve", neglam.v, neglam.v, -lambda_init, None, ALU.add)
            kb.ts("dve", gsub.v, gsub.v, 1.0 - lambda_init)
            qhs = Ring([kb.sbuf([128, Lm], BF16, "qh%d" % i, ph) for i in range(2)])
            khs = Ring([kb.sbuf([128, Lm], BF16, "kh%d" % i, ph) for i in range(2)])
            vts = Ring([kb.sbuf([128, NTm, 130], BF16, "vt%d" % i, ph) for i in range(2)])
            for vt in vts.items:
                kb.memset("pool", vt[:, :, 128:129], 1.0)
                kb.memset("pool", vt[:, :, 129:130], 0.0)
            oTs = Ring([kb.sbuf([128, Lm], BF16, "oT%d" % i, ph) for i in range(2)])
            ets = Ring([kb.sbuf([128, 512], BF16, "et%d" % i, ph) for i in range(3)])
            o0 = kb.sbuf([128, 4, 128], F32, "o0", ph)
            ods = Ring([kb.sbuf([128, 128], F32, "od%d" % i, ph) for i in range(2)])
            ons = Ring([kb.sbuf([128, 128], BF16, "on%d" % i, ph) for i in range(2)])
            sqj = kb.sbuf([128, 128], F32, "sqj", ph)
            smalls = Ring([kb.sbuf([128, 4], F32, "sm%d" % i, ph) for i in range(4)])
            psS = Ring([kb.psum([128, 512], F32, "psS%d" % i, ph) for i in range(3)])
            psO = Ring([kb.psum([128, 2, 256], F32, "psO%d" % i, ph) for i in range(4)])
            pstb = kb.psum([128, 8, 128], BF16, "psT", ph)
            psT = Ring([sub(pstb, (slice(None), i, slice(None))) for i in range(8)])
            for s, L in enumerate(self.seqs):
                off = self.offs[s]
                NT = L // 128
                for j in range(8):
                    qh, kh, vt, oT = qhs.next(), khs.next(), vts.next(), oTs.next()
                    kb.dma("sp", qh[:, :L], self.QT[j, :, off:off + L])
                    kb.dma("sp", kh[:, :L], self.KT[j, :, off:off + L])
                    kb.dma("sp", vt[:, :NT, 0:128], self.Vd[off:off + L, j * 128:(j + 1) * 128].rearrange("(a p) e -> p a e", p=128))
                    for qt in range(L // 512):
                        for c in range(2):
                            oa = [psO.next(), psO.next()]
                            def smm(kt_):
                                p_ = psS.next()
                                kb.mm(p_.v, kh[c * 64:(c + 1) * 64, kt_ * 128:(kt_ + 1) * 128],
                                      qh[c * 64:(c + 1) * 64, qt * 512:(qt + 1) * 512])
                                return p_
                            pS_next = smm(0)
                            for kt in range(NT):
                                pS = pS_next
                                if kt + 1 < NT:
                                    pS_next = smm(kt + 1)
                                et = ets.next()
                                kb.act(et.v, pS.v, AF.Exp, scale=0.125)
                                for qs in range(4):
                                    kb.mm(oa[qs // 2][:, qs % 2, 0:130], et[:, qs * 128:(qs + 1) * 128], vt[:, kt, :],
                                          start=(kt == 0 and qs % 2 == 0), stop=(kt == NT - 1), nocheck=True)
                            for qs in range(4):
                                acc = oa[qs // 2]
                                sm = smalls.next()
                                kb.recip(sm[:, 0:1], acc[:, qs % 2, 128:129])
                                if c == 0:
                                    kb.ts("dve", o0[:, qs, :], acc[:, qs % 2, 0:128], sm[:, 0:1])
                                else:
                                    od = ods.next()
                                    on = ons.next()
                                    kb.tt("dve", sm[:, 1:2], sm[:, 0:1], neglam.v, ALU.mult)
                                    kb.stt("dve", od.v, acc[:, qs % 2, 0:128], sm[:, 1:2], o0[:, qs, :], ALU.mult, ALU.add)
                                    kb.act(sqj.v, od.v, AF.Square, accum_out=sm[:, 2:3])
                                    kb.act(sm[:, 3:4], sm[:, 2:3], AF.Sqrt, scale=1.0 / 128.0, bias=self.epsT.v)
                                    kb.recip(sm[:, 3:4], sm[:, 3:4])
                                    kb.stt("dve", on.v, od.v, sm[:, 3:4], gsub.v, ALU.mult, ALU.mult)
                                    pT = psT.next()
                                    kb.transpose(pT.v, on.v, self.identb.v)
                                    kb.copy("act", oT[:, qt * 512 + qs * 128: qt * 512 + (qs + 1) * 128], pT.v)
                    kb.dma("pool", self.OT[j, :, off:off + L], oT[:, :L])
            kb.barrier()


    def core_na(self, li):
        kb = self.kb
        Lm = self.Lmax
        NTm = Lm // 128
        rp = self.din["na_rpbp"]
        with ExitStack() as ph:
            mk = kb.sbuf([128, 2, 14, 64], F32, "namask", ph)
            kb.dma("sp", mk.v, self.c_namask.rearrange("a p d q -> p a d q"))
            BTs = Ring([kb.sbuf([128, 14, 64], F32, "BT%d" % i, ph) for i in range(2)])
            H15s = Ring([kb.sbuf([64, 15, 64], F32, "H15%d" % i, ph) for i in range(2)])
            antiI = kb.sbuf([64, 64], F32, "antiI", ph)
            kb.dma("sp", antiI.v, self.c_antiI[:, :])
            psBias = Ring([kb.psum([128, 512], F32, "npB%d" % i, ph) for i in range(1)])
            qhs = Ring([kb.sbuf([64, Lm], BF16, "nq%d" % i, ph) for i in range(2)])
            khs = Ring([kb.sbuf([64, Lm], BF16, "nk%d" % i, ph) for i in range(2)])
            v0s = Ring([kb.sbuf([128, NTm, 66], BF16, "nv0%d" % i, ph) for i in range(2)])
            v1s = Ring([kb.sbuf([128, NTm, 66], BF16, "nv1%d" % i, ph) for i in range(2)])
            for vt in v0s.items + v1s.items:
                kb.memset("pool", vt[:, :, 64:65], 1.0)
                kb.memset("pool", vt[:, :, 65:66], 0.0)
            onTs = Ring([kb.sbuf([64, Lm], BF16, "nonT%d" % i, ph) for i in range(2)])
            sbs = Ring([kb.sbuf([128, 2, 4, 64], F32, "nsb%d" % i, ph) for i in range(2)])
            ets = Ring([kb.sbuf([128, 2, 4, 64], BF16, "net%d" % i, ph) for i in range(3)])
            onts = Ring([kb.sbuf([128, 64], BF16, "nont%d" % i, ph) for i in range(2)])
            sms = Ring([kb.sbuf([128, 1], F32, "nsm%d" % i, ph) for i in range(3)])
            psS = Ring([kb.psum([128, 2, 4, 64], F32, "npS%d" % i, ph) for i in range(3)])
            psO = Ring([kb.psum([128, 512], F32, "npO%d" % i, ph) for i in range(3)])
            pstb = kb.psum([128, 8, 128], BF16, "npT", ph)
            psT = Ring([sub(pstb, (slice(0, 64), i, slice(None))) for i in range(8)])
            for hd in range(16):
                BT = BTs.next()
                H15 = H15s.next()
                kb.dma("sp", H15.v, bass.AP(rp, 64 + hd * 15 * 31 - 48, [[1, 64], [31, 15], [1, 64]]))
                for g_ in range(2):
                    pB = psBias.next()
                    for dd_ in range(7):
                        d0 = g_ * 7 + dd_
                        kb.mm(pB[:, dd_ * 64:(dd_ + 1) * 64], H15[:, d0:d0 + 2, :].re("p a k -> p (a k)"), antiI.v)
                    kb.copy("dve", BT[:, g_ * 7:(g_ + 1) * 7, :], pB[:, 0:448].re("p (a q) -> p a q", q=64))
                kb.tt("dve", BT.v, BT.v, mk[:, 0], ALU.mult)
                kb.tt("dve", BT.v, BT.v, mk[:, 1], ALU.add)
                cq, rq = hd // 2, (hd % 2) * 64
                for s, L in enumerate(self.seqs):
                    off = self.offs[s]
                    rows, NT = L // 64, L // 128
                    qh, kh, v0, v1, onT = qhs.next(), khs.next(), v0s.next(), v1s.next(), onTs.next()
                    kb.dma("sp", qh[:, :L], self.QT[cq, rq:rq + 64, off:off + L])
                    kb.dma("sp", kh[:, :L], self.KT[cq, rq:rq + 64, off:off + L])
                    kb.dma("sp", v0[:, :NT, 0:64], self.Vd[off:off + L, hd * 64:(hd + 1) * 64].rearrange("(a p) e -> p a e", p=128))
                    kb.dma("sp", v1[:, :NT - 1, 0:64], self.Vd[off + 64:off + L - 64, hd * 64:(hd + 1) * 64].rearrange("(a p) e -> p a e", p=128))
                    def stage_a(r2_):
                        pS_ = psS.next()
                        info_ = []
                        for rr in range(2):
                            r = 2 * r2_ + rr
                            rs_ = min(max(r - 4, 0), rows - 8)
                            b0 = rs_ * 64
                            for i in range(4):
                                kb.mm(pS_[:, rr, i, :], kh[:, b0 + 128 * i:b0 + 128 * (i + 1)], qh[:, r * 64:(r + 1) * 64])
                            info_.append((rs_, rs_ - r + 7))
                        return pS_, info_
                    nxt = stage_a(0)
                    for r2 in range(rows // 2):
                        pS, info = nxt
                        if r2 + 1 < rows // 2:
                            nxt = stage_a(r2 + 1)
                        sb = sbs.next()
                        for rr in range(2):
                            base = info[rr][1]
                            kb.stt("dve", sb[:, rr], pS[:, rr], 0.125, BT[:, base:base + 7:2, :], ALU.mult, ALU.add)
                        et = ets.next()
                        kb.act(et.v, sb.v, AF.Exp)
                        pO = psO.next()
                        for rr in range(2):
                            rs_ = info[rr][0]
                            for i in range(4):
                                vt = v0[:, rs_ // 2 + i, :] if rs_ % 2 == 0 else v1[:, (rs_ - 1) // 2 + i, :]
                                kb.mm(pO[rr * 64:(rr + 1) * 64, 0:66], et[:, rr, i, :], vt, start=(i == 0), stop=(i == 3))
                        sm = sms.next()
                        kb.recip(sm.v, pO[:, 64:65])
                        ont = onts.next()
                        kb.ts("dve", ont.v, pO[:, 0:64], sm.v)
                        pT = psT.next()
                        kb.transpose(pT.v, ont.v, self.identb.v)
                        kb.copy("act", onT[:, r2 * 128:(r2 + 1) * 128], pT.v)
                    kb.dma("pool", self.OT[cq, rq:rq + 64, off:off + L], onT[:, :L])
            kb.barrier()


    def core_s5(self, li):
        kb = self.kb
        SEG = min(512, min(self.seqs))
        NLEV = int(round(math.log2(SEG)))
        Lm = self.Lmax
        PI = math.pi
        with ExitStack() as ph:
            BW = kb.sbuf([128, 2, 8, 4, 2, 128], BF16, "BW", ph)
            CW = kb.sbuf([128, 2, 32, 2, 128], BF16, "CW", ph)
            LP = kb.sbuf([128, 2, NLEV, 32, 3], F32, "LP", ph)
            dT = kb.sbuf([128, 8], F32, "s5d", ph)
            negpi = kb.sbuf([128, 1], F32, "negpi", ph)
            kb.memset("dve", negpi, -PI)
            kb.dma("sp", dT.v, self.s5_dT[:, :])
            kb.memset("pool", CW, 0.0)

            def lambar(st, are, aim, dte, shape, name):
                lr = kb.sbuf(shape, F32, name + "lr", st)
                lim = kb.sbuf(shape, F32, name + "li", st)
                t = kb.sbuf(shape, F32, name + "t", st)
                kb.tt("dve", lr.v, are, dte, ALU.mult)
                kb.tt("dve", lim.v, aim, dte, ALU.mult)
                kb.act(lr.v, lr.v, AF.Exp)
                sn = kb.sbuf(shape, F32, name + "sn", st)
                cs = kb.sbuf(shape, F32, name + "cs", st)
                for dst, shift in ((sn, 0.0), (cs, 0.5 * PI)):
                    kb.ts("dve", t.v, lim.v, shift, None, ALU.add)
                    for _ in range(4):
                        kb.ts("dve", dst.v, t.v, PI, -2 * PI, ALU.is_gt, ALU.mult)
                        kb.tt("dve", t.v, t.v, dst.v, ALU.add)
                    kb.act(dst.v, t.v, AF.Sin)
                kb.tt("dve", cs.v, cs.v, lr.v, ALU.mult)
                kb.tt("dve", sn.v, sn.v, lr.v, ALU.mult)
                return cs, sn

            with ExitStack() as st:
                rmask = kb.sbuf([128, 8], F32, "rmask", st)
                kb.dma("sp", rmask.v, self.c_rowmask[:, :])
                for d in range(2):
                    are = kb.sbuf([128, 8, 64], F32, "are%d" % d, st)
                    aim = kb.sbuf([128, 8, 64], F32, "aim%d" % d, st)
                    bre = kb.sbuf([128, 8, 64], F32, "bre%d" % d, st)
                    bim = kb.sbuf([128, 8, 64], F32, "bim%d" % d, st)
                    dte = kb.sbuf([128, 8, 64], F32, "dte%d" % d, st)
                    dt0 = kb.sbuf([128, 8], F32, "dt0%d" % d, st)
                    kb.dma("sp", are.v, self.s5_aA[0, d])
                    kb.dma("sp", aim.v, self.s5_aA[1, d])
                    kb.dma("sp", bre.v, self.s5_bA[0, d])
                    kb.dma("sp", bim.v, self.s5_bA[1, d])
                    kb.dma("sp", dt0.v, self.s5_dtA[d])
                    kb.act(dt0.v, dt0.v, AF.Exp)
                    for ck in range(8):
                        kb.ts("dve", dte[:, ck, :], self.onesf[:, 0:64], dt0[:, ck:ck + 1])
                    lbr, lbi = lambar(st, are.v, aim.v, dte.v, [128, 8, 64], "A%d" % d)
                    den = kb.sbuf([128, 8, 64], F32, "den%d" % d, st)
                    t1 = kb.sbuf([128, 8, 64], F32, "t1%d" % d, st)
                    fr = kb.sbuf([128, 8, 64], F32, "fr%d" % d, st)
                    fi = kb.sbuf([128, 8, 64], F32, "fi%d" % d, st)
                    kb.ts("dve", lbr.v, lbr.v, -1.0, None, ALU.add)
                    kb.tt("dve", den.v, are.v, are.v, ALU.mult)
                    kb.tt("dve", t1.v, aim.v, aim.v, ALU.mult)
                    kb.tt("dve", den.v, den.v, t1.v, ALU.add)
                    kb.recip(den.v, den.v)
                    kb.tt("dve", fr.v, lbr.v, are.v, ALU.mult)
                    kb.tt("dve", t1.v, lbi.v, aim.v, ALU.mult)
                    kb.tt("dve", fr.v, fr.v, t1.v, ALU.add)
                    kb.tt("dve", fr.v, fr.v, den.v, ALU.mult)
                    kb.tt("dve", fi.v, lbi.v, are.v, ALU.mult)
                    kb.tt("dve", t1.v, lbr.v, aim.v, ALU.mult)
                    kb.tt("dve", fi.v, fi.v, t1.v, ALU.subtract)
                    kb.tt("dve", fi.v, fi.v, den.v, ALU.mult)
                    Br = kb.sbuf([128, 8, 64], F32, "Br%d" % d, st)
                    Bi = kb.sbuf([128, 8, 64], F32, "Bi%d" % d, st)
                    kb.tt("dve", Br.v, fr.v, bre.v, ALU.mult)
                    kb.tt("dve", t1.v, fi.v, bim.v, ALU.mult)
                    kb.tt("dve", Br.v, Br.v, t1.v, ALU.subtract)
                    kb.tt("dve", Bi.v, fr.v, bim.v, ALU.mult)
                    kb.tt("dve", t1.v, fi.v, bre.v, ALU.mult)
                    kb.tt("dve", Bi.v, Bi.v, t1.v, ALU.add)
                    for ck in range(8):
                        for j in range(4):
                            for ri, Bx in enumerate((Br, Bi)):
                                for hf in range(2):
                                    kb.ts("dve", BW[:, d, ck, j, ri, hf * 64:(hf + 1) * 64], Bx[:, ck, :],
                                          rmask[:, 2 * j + hf:2 * j + hf + 1])
                    ar2 = kb.sbuf([128, 32], F32, "ar2%d" % d, st)
                    ai2 = kb.sbuf([128, 32], F32, "ai2%d" % d, st)
                    dt2 = kb.sbuf([128, 32], F32, "dt2%d" % d, st)
                    kb.dma("sp", ar2.v, self.s5_aB[0, d])
                    kb.dma("sp", ai2.v, self.s5_aB[1, d])
                    kb.dma("sp", dt2.v, self.s5_dtB[d])
                    kb.act(dt2.v, dt2.v, AF.Exp)
                    pr, pi_ = lambar(st, ar2.v, ai2.v, dt2.v, [128, 32], "B%d" % d)
                    t2 = kb.sbuf([128, 32], F32, "t2%d" % d, st)
                    t3 = kb.sbuf([128, 32], F32, "t3%d" % d, st)
                    for lev in range(NLEV):
                        kb.copy("dve", LP[:, d, lev, :, 0], pr.v)
                        kb.copy("dve", LP[:, d, lev, :, 1], pi_.v)
                        kb.ts("dve", LP[:, d, lev, :, 2], pi_.v, -1.0)
                        if lev < NLEV - 1:
                            kb.tt("dve", t2.v, pr.v, pr.v, ALU.mult)
                            kb.tt("dve", t3.v, pi_.v, pi_.v, ALU.mult)
                            kb.tt("dve", pi_.v, pr.v, pi_.v, ALU.mult)
                            kb.ts("dve", pi_.v, pi_.v, 2.0)
                            kb.tt("dve", pr.v, t2.v, t3.v, ALU.subtract)
                    for ri in range(2):
                        cb = kb.sbuf([128, 32, 16], F32, "cb%d%d" % (d, ri), st)
                        kb.dma("sp", cb.v, self.s5_cB[ri, d])
                        sgn = 1.0 if ri == 0 else -1.0
                        for j in range(4):
                            kb.act(CW[0:64, d, j::4, ri, 32 * j:32 * j + 16], cb[0:64, j::4, :], AF.Copy, scale=sgn)
                            kb.act(CW[64:128, d, j::4, ri, 32 * j + 16:32 * j + 32], cb[64:128, j::4, :], AF.Copy, scale=sgn)
                kb.barrier()

            ufs = Ring([kb.sbuf([128, Lm], F32, "uf%d" % i, ph) for i in range(1)])
            ubs = Ring([kb.sbuf([128, Lm], BF16, "ub%d" % i, ph) for i in range(1)])
            ych = kb.sbuf([128, Lm], F32, "ych", ph)
            gos = Ring([kb.sbuf([128, Lm], BF16, "go%d" % i, ph) for i in range(1)])
            PAD = SEG // 2
            sets = Ring([[kb.sbuf([128, 2, SEG + 2 * PAD], F32, "st%d_%d" % (i, q), ph) for q in range(2)] for i in range(4)])
            for st_ in sets.items:
                for tl_ in st_:
                    kb.memset("pool", tl_, 0.0)
            xbs = Ring([[kb.sbuf([128, SEG], BF16, "xb%d_%d" % (i, q), ph) for q in range(2)] for i in range(4)])
            carries = [[kb.sbuf([128, 2], F32, "car%d_%d" % (d, j), ph) for j in range(4)] for d in range(2)]
            psB = Ring([kb.psum([128, 512], F32, "s5pB%d" % i, ph) for i in range(4)])
            psY = Ring([kb.psum([128, 512], F32, "s5pY%d" % i, ph) for i in range(4)])
            NT5 = SEG // 512
            unit = [0]
            for s, L in enumerate(self.seqs):
                off = self.offs[s]
                nseg = L // SEG
                for ck in range(8):
                    uf, ub, go = ufs.next(), ubs.next(), gos.next()
                    kb.dma("sp", uf[:, :L], self.UT[ck, :, off:off + L])
                    kb.copy("act", ub[:, :L], uf[:, :L])
                    for d in range(2):
                        order = range(nseg) if d == 0 else range(nseg - 1, -1, -1)
                        for si, seg in enumerate(order):
                            t0 = seg * SEG
                            pys = [psY.next() for _ in range(NT5)]
                            for jj in (0, 2):
                                units = []
                                for j in (jj, jj + 1):
                                    gp = ck * 4 + j
                                    Az, Bz = sets.next()
                                    xbr, xbi = xbs.next()
                                    car = carries[d][j]
                                    for tt_ in range(NT5):
                                        for ri in range(2):
                                            pb = psB.next()
                                            kb.mm(pb.v, BW[:, d, ck, j, ri, :], ub[:, t0 + tt_ * 512:t0 + (tt_ + 1) * 512])
                                            kb.copy("act", Az[:, ri, PAD + tt_ * 512:PAD + (tt_ + 1) * 512], pb.v)
                                    fcol = PAD if d == 0 else PAD + SEG - 1
                                    if si > 0:
                                        a0 = LP[:, d, 0, gp, 0:1]
                                        b0 = LP[:, d, 0, gp, 1:2]
                                        nb0 = LP[:, d, 0, gp, 2:3]
                                        f = slice(fcol, fcol + 1)
                                        kb.stt("dve", Az[:, 0, f], car[:, 0:1], a0, Az[:, 0, f], ALU.mult, ALU.add)
                                        kb.stt("dve", Az[:, 1, f], car[:, 1:2], a0, Az[:, 1, f], ALU.mult, ALU.add)
                                        kb.stt("dve", Az[:, 0, f], car[:, 1:2], nb0, Az[:, 0, f], ALU.mult, ALU.add)
                                        kb.stt("dve", Az[:, 1, f], car[:, 0:1], b0, Az[:, 1, f], ALU.mult, ALU.add)
                                    units.append([gp, j, Az, Bz, car, xbr, xbi])
                                o_ = slice(PAD, PAD + SEG)
                                for lev in range(NLEV):
                                    dd = 1 << lev
                                    sh = slice(PAD - dd, PAD + SEG - dd) if d == 0 else slice(PAD + dd, PAD + SEG + dd)
                                    for u in units:
                                        gp, j, sz, dz = u[0:4]
                                        a = LP[:, d, lev, gp, 0:1]
                                        kb.stt("dve", dz[:, :, o_], sz[:, :, sh], a, sz[:, :, o_], ALU.mult, ALU.add)
                                    for u in units:
                                        gp, j, sz, dz = u[0:4]
                                        nb = LP[:, d, lev, gp, 2:3]
                                        kb.stt("dve", dz[:, 0, o_], sz[:, 1, sh], nb, dz[:, 0, o_], ALU.mult, ALU.add)
                                    for u in units:
                                        gp, j, sz, dz = u[0:4]
                                        b = LP[:, d, lev, gp, 1:2]
                                        kb.stt("dve", dz[:, 1, o_], sz[:, 0, sh], b, dz[:, 1, o_], ALU.mult, ALU.add)
                                        u[2], u[3] = dz, sz
                                lcol = PAD + SEG - 1 if d == 0 else PAD
                                for u in units:
                                    gp, j, sz, dz, car, xbr, xbi = u
                                    kb.copy("pool", car[:, 0:1], sz[:, 0, lcol:lcol + 1])
                                    kb.copy("pool", car[:, 1:2], sz[:, 1, lcol:lcol + 1])
                                    kb.copy("act", xbr.v, sz[:, 0, PAD:PAD + SEG])
                                    kb.copy("act", xbi.v, sz[:, 1, PAD:PAD + SEG])
                                    for tt_ in range(NT5):
                                        kb.mm(pys[tt_].v, CW[:, d, gp, 0, :], xbr[:, tt_ * 512:(tt_ + 1) * 512],
                                              start=(j == 0), stop=False)
                                        kb.mm(pys[tt_].v, CW[:, d, gp, 1, :], xbi[:, tt_ * 512:(tt_ + 1) * 512],
                                              start=False, stop=(j == 3))
                            for tt_ in range(NT5):
                                sl = slice(t0 + tt_ * 512, t0 + (tt_ + 1) * 512)
                                if d == 0:
                                    kb.copy("act", ych[:, sl], pys[tt_].v)
                                else:
                                    kb.tt("dve", ych[:, sl], ych[:, sl], pys[tt_].v, ALU.add)
                    kb.stt("dve", ych[:, :L], uf[:, :L], dT[:, ck:ck + 1], ych[:, :L], ALU.mult, ALU.add)
                    kb.act(go[:, :L], ych[:, :L], AF.Gelu)
                    kb.dma("pool", self.OT[ck, :, off:off + L], go[:, :L])
            kb.barrier()


    def core_dn(self, li):
        kb = self.kb
        Lm = self.Lmax
        NTm = Lm // 128
        with ExitStack() as ph:
            masks = kb.sbuf([128, 4, 128], F32, "dmask", ph)
            kb.dma("sp", masks.v, self.c_masks.rearrange("a p q -> p a q"))
            convw = kb.sbuf([128, 24, 4], F32, "convw", ph)
            kb.dma("sp", convw.v, self.dn_convT[:, :, :])
            onormb = kb.sbuf([128, 128], F32, "onormb", ph)
            kb.dma("sp", onormb.v, self.dn_onorm[0].partition_broadcast(128))
            dtbc = kb.sbuf([16, 1], F32, "dtbc", ph)
            negAc = kb.sbuf([16, 1], F32, "negAc", ph)
            kb.dma("sp", dtbc.v, self.dn_dtb.rearrange("a b -> b a"), allow_slow_non_contiguous=True)
            kb.dma("sp", negAc.v, self.dn_alog.rearrange("a b -> b a"), allow_slow_non_contiguous=True)
            kb.act(negAc.v, negAc.v, AF.Exp)
            kb.ts("dve", negAc.v, negAc.v, -1.0)
            Gt = kb.sbuf([128, NTm, 16], F32, "Gt", ph)
            BETA = kb.sbuf([128, NTm, 16], F32, "BETA", ph)
            NB = kb.sbuf([128, NTm, 16], F32, "NB", ph)
            GCt = kb.sbuf([128, NTm, 16], F32, "GCt", ph)
            CD = kb.sbuf([128, NTm, 16], F32, "CD", ph)
            EGL = kb.sbuf([128, NTm, 16], F32, "EGL", ph)
            BK = kb.sbuf([128, NTm, 16], F32, "BK", ph)
            abT = Ring([kb.sbuf([32, 512], F32, "abT%d" % i, ph) for i in range(1)])
            sgT = Ring([kb.sbuf([32, 512], F32, "sgT%d" % i, ph) for i in range(1)])
            gT = Ring([kb.sbuf([16, 512], F32, "gT%d" % i, ph) for i in range(1)])
            xin = kb.sbuf([128, Lm + 3], F32, "xin", ph)
            qT = kb.sbuf([128, Lm], F32, "dqT", ph)
            kT = kb.sbuf([128, Lm], F32, "dkT", ph)
            zs = kb.sbuf([128, Lm], F32, "dzs", ph)
            vtm = kb.sbuf([128, NTm, 128], F32, "vtm", ph)
            oacc = kb.sbuf([128, NTm, 128], F32, "oacc", ph)
            oT = kb.sbuf([128, Lm], BF16, "doT", ph)
            sqb = Ring([kb.sbuf([128, 512], BF16, "dsq%d" % i, ph) for i in range(2)])
            rnb = Ring([kb.sbuf([128, 512], F32, "drn%d" % i, ph) for i in range(2)])
            Sst = [kb.sbuf([128, 128], F32, "dS%d" % d, ph) for d in range(2)]

            def ring(name, shape, n=2, dt=F32):
                return [Ring([kb.sbuf(shape, dt, "%s%d_%d" % (name, d, i), ph) for i in range(n)]) for d in range(2)]
            rGU, rT1, rDm, rDT, rEG, rG0 = (ring(nm, [128, 128]) for nm in ("GU", "T1", "Dm", "DT", "EG", "G0"))
            rqd, rAT, rwT, rkd, rvn = (ring(nm, [128, 128]) for nm in ("qd", "AT", "wT", "kd", "vn"))
            rP = ring("P", [128, 128], 3)
            rPT = ring("PT", [128, 128], 3)
            rLT = ring("LT", [128, 128], 2)
            rkt = ring("kt", [128, 128], 2)
            rLL = ring("LL", [128, 128], 2)
            rY2 = ring("Y2", [128, 128], 2)
            rP0 = ring("P0", [128, 128], 2)
            rPT0 = ring("PT0", [128, 128], 2)
            rY = ring("Y", [128, 128], 2)
            lmask = kb.sbuf([128, 2, 7, 128], F32, "lmask", ph)
            kb.dma("sp", lmask.v, self.c_lmask.rearrange("a l p q -> p a l q"))
            rX = ring("X", [128, 256], 3)
            sm = Ring([kb.sbuf([128, 2], F32, "dsm%d" % i, ph) for i in range(3)])
            onr = Ring([kb.sbuf([128, 128], F32, "don%d" % i, ph) for i in range(2)])
            sqj = kb.sbuf([128, 128], F32, "dsqj", ph)
            pss = Ring([kb.psum([128, 512], F32, "dps%d" % i, ph) for i in range(8)])
            ev = [0]

            import os as _os
            _evm = _os.environ.get("DN_EV", "alt")

            def evac(out, ps, eng=None):
                if eng is None:
                    ev[0] += 1
                    eng = "act" if ev[0] % 3 else "dve"
                kb.copy(eng, out, ps)
                return eng

            for s, L in enumerate(self.seqs):
                off = self.offs[s]
                NT = L // 128
                for t0 in range(0, L, 512):
                    ab, sg, g_ = abT.next(), sgT.next(), gT.next()
                    kb.dma("sp", ab.v, self.ABd[:, off + t0:off + t0 + 512])
                    kb.act(sg.v, ab.v, AF.Sigmoid)
                    kb.act(g_.v, ab[0:16, :], AF.Exp, scale=1.0, bias=dtbc.v)
                    kb.act(g_.v, g_.v, AF.Ln, scale=1.0, bias=self.oneT[0:16, :])
                    kb.ts("dve", g_.v, g_.v, negAc.v)
                    for a in range(4):
                        n = t0 // 128 + a
                        p1 = pss.next()
                        kb.transpose(p1[:, 0:16], g_[:, a * 128:(a + 1) * 128], self.ident[0:16, 0:16])
                        kb.transpose(p1[:, 16:48], sg[:, a * 128:(a + 1) * 128], self.ident[0:32, 0:32])
                        kb.copy("dve", Gt[:, n, :], p1[:, 0:16])
                        kb.copy("dve", BETA[:, n, :], p1[:, 32:48])
                kb.ts("dve", NB[:, :NT, :], BETA[:, :NT, :], -1.0)
                p1 = pss.next()
                kb.mm(p1[:, 0:NT * 8].re("p (n h) -> p n h", h=8), masks[:, 2, :], Gt[:, :NT, 0:8])
                kb.mm(p1[:, 256:256 + NT * 8].re("p (n h) -> p n h", h=8), masks[:, 0, :], Gt[:, :NT, 8:16])
                kb.copy("dve", GCt[:, :NT, 0:8], p1[:, 0:NT * 8].re("p (n h) -> p n h", h=8))
                kb.copy("dve", GCt[:, :NT, 8:16], p1[:, 256:256 + NT * 8].re("p (n h) -> p n h", h=8))
                p2 = pss.next()
                kb.mm(p2[:, 0:NT * 16].re("p (n h) -> p n h", h=16), self.onesf.v, Gt[:, :NT, :])
                p2v = p2[:, 0:NT * 16].re("p (n h) -> p n h", h=16)
                kb.copy("dve", CD[:, :NT, :], p2v)
                kb.tt("dve", EGL[:, :NT, :], CD[:, :NT, :], GCt[:, :NT, :], ALU.subtract)
                kb.act(CD[:, :NT, :], CD[:, :NT, :], AF.Exp)
                kb.act(EGL[:, :NT, :], EGL[:, :NT, :], AF.Exp)
                kb.act(BK[:, :NT, :], GCt[:, :NT, :], AF.Exp)
                kb.tt("dve", BK[:, :NT, :], BK[:, :NT, :], BETA[:, :NT, :], ALU.mult)
                stop = self.cfg.get("dn_stop", 99)
                if stop <= 2:
                    continue

                for hd in range(int(self.cfg.get('dn_heads', 8))):
                    kb.memset("pool", xin[:, 0:1], 0.0)
                    kb.memset("pool", xin[:, L + 1:L + 3], 0.0)
                    for which, chunk, dest in (("q", hd, qT), ("k", 8 + hd, kT), ("v", 16 + hd, zs)):
                        kb.dma("sp", xin[:, 1:L + 1], self.PRE[chunk, :, off:off + L])
                        kb.ts("pool", dest[:, :L], xin[:, 0:L], convw[:, chunk, 0:1])
                        for k_ in range(1, 4):
                            kb.stt("dve", dest[:, :L], xin[:, k_:k_ + L], convw[:, chunk, k_:k_ + 1], dest[:, :L], ALU.mult, ALU.add)
                        kb.act(dest[:, :L], dest[:, :L], AF.Silu)
                        if which != "v":
                            scale = (128.0 ** -0.5) if which == "q" else 1.0
                            for t0 in range(0, L, 512):
                                sq, rn = sqb.next(), rnb.next()
                                kb.act(sq.v, dest[:, t0:t0 + 512], AF.Square)
                                p1 = pss.next()
                                kb.mm(p1.v, self.ones1b.v, sq.v)
                                kb.act(rn.v, p1.v, AF.Sqrt, scale=1.0, bias=self.epsT.v)
                                kb.recip(rn.v, rn.v)
                                kb.stt("dve", dest[:, t0:t0 + 512], dest[:, t0:t0 + 512], scale, rn.v, ALU.mult, ALU.mult)
                    if stop <= 3:
                        continue
                    for n in range(NT):
                        p1 = pss.next()
                        kb.transpose(p1[:, 128:256], zs[:, n * 128:(n + 1) * 128], self.ident.v)
                        evac(vtm[:, n, :], p1[:, 128:256])
                    if stop <= 3.5:
                        continue
                    kb.dma("sp", zs[:, :L], self.PRE[24 + hd, :, off:off + L])
                    kb.act(zs[:, :L], zs[:, :L], AF.Silu)
                    if stop <= 3.7:
                        continue
                    kb.memset("pool", oacc[:, :NT, :], 0.0)
                    for d in range(2):
                        kb.memset("pool", Sst[d], 0.0)
                    if stop <= 4:
                        continue
                    _dirs = [int(x) for x in _os.environ.get("DN_DIR", "01")]
                    def unit(d, n):
                        kb.f32r = bool(int(_os.environ.get("DN_F32R", "0")))
                        tl = slice(n * 128, (n + 1) * 128)
                        col = d * 8 + hd
                        g1 = Gt[:, n, col:col + 1]
                        gc = GCt[:, n, col:col + 1]
                        CUM = masks[:, 2 if d == 0 else 0, :]
                        Mstr = masks[:, 1 if d == 0 else 3, :]
                        Msc = masks[:, 2 if d == 0 else 0, :]
                        GU = rGU[d].next()
                        kb.ts("pool", GU.v, CUM, g1)
                        yield
                        pG = pss.next()
                        kb.mm(pG[:, 0:128], self.onesf.v, GU.v)
                        yield
                        G0 = rG0[d].next()
                        kb.copy("dve", G0.v, pG[:, 0:128])
                        yield
                        T1 = rT1[d].next()
                        Dm = rDm[d].next()
                        kb.ts("dve", T1.v, G0.v, gc, 0.0, ALU.subtract, ALU.max)
                        yield
                        kb.act(Dm.v, T1.v, AF.Exp, scale=-1.0)
                        yield
                        kb.stt("dve", Dm.v, Dm.v, NB[:, n, col:col + 1], Mstr, ALU.mult, ALU.mult)
                        yield
                        T2 = rT1[d].next()
                        DT = rDT[d].next()
                        kb.ts("dve", T2.v, G0.v, gc, 0.0, ALU.subtract, ALU.min)
                        yield
                        kb.act(DT.v, T2.v, AF.Exp)
                        yield
                        kb.tt("pool", DT.v, DT.v, Msc, ALU.mult)
                        yield
                        EG = rEG[d].next()
                        kb.act(EG.v, G0.v, AF.Exp)
                        yield
                        qd = rqd[d].next()
                        kb.tt("dve", qd.v, qT[:, tl], EG.v, ALU.mult)
                        yield
                        pK = pss.next()
                        kb.mm(pK[:, 0:128], kT[:, tl], kT[:, tl])
                        yield
                        kb.mm(pK[:, 128:256], kT[:, tl], qT[:, tl])
                        yield
                        P = rP0[d].next()
                        kb.tt("dve", P.v, pK[:, 0:128], Dm.v, ALU.mult)
                        yield
                        AT = rAT[d].next()
                        kb.tt("dve", AT.v, pK[:, 128:256], DT.v, ALU.mult)
                        yield
                        pT_ = pss.next()
                        kb.transpose(pT_[:, 0:128], P.v, self.ident.v)
                        yield
                        PT = rPT0[d].next()
                        evac(PT.v, pT_[:, 0:128])
                        yield
                        pKt = pss.next()
                        kb.transpose(pKt[:, 0:128], kT[:, tl], self.ident.v)
                        yield
                        ktm_t = rkt[d].next()
                        evac(ktm_t.v, pKt[:, 0:128])
                        yield
                        R_ = rX[d].next()
                        kb.ts("pool", R_[:, 0:128], vtm[:, n, :], BETA[:, n, col:col + 1])
                        yield
                        kb.ts("pool", R_[:, 128:256], ktm_t.v, BK[:, n, col:col + 1])
                        yield
                        lm = 0 if d == 0 else 1
                        Tm = rP[d].next()
                        TTm = rPT[d].next()
                        kb.tt("pool", Tm.v, P.v, lmask[:, lm, 0, :], ALU.mult)
                        yield
                        kb.tt("pool", Tm.v, Tm.v, self.ident.v, ALU.add)
                        yield
                        kb.tt("pool", TTm.v, PT.v, lmask[:, 1 - lm, 0, :], ALU.mult)
                        yield
                        kb.tt("pool", TTm.v, TTm.v, self.ident.v, ALU.add)
                        yield
                        for lev in range(1, 7):
                            LT = rLT[d].next()
                            kb.tt("pool", LT.v, PT.v, lmask[:, 1 - lm, lev, :], ALU.mult)
                            yield
                            L_ = rLL[d].next()
                            kb.tt("pool", L_.v, P.v, lmask[:, lm, lev, :], ALU.mult)
                            yield
                            pY = pss.next()
                            kb.mm(pY[:, 0:128], LT.v, Tm.v)
                            yield
                            kb.mm(pY[:, 128:256], L_.v, TTm.v)
                            yield
                            Y_ = rY[d].next()
                            Y2 = rY2[d].next()
                            e_ = evac(Y_.v, pY[:, 0:128])
                            yield
                            evac(Y2.v, pY[:, 128:256], eng=e_)
                            yield
                            pZ = pss.next()
                            kb.mm(pZ[:, 0:128], TTm.v, Y_.v)
                            yield
                            kb.mm(pZ[:, 128:256], Tm.v, Y2.v)
                            yield
                            Tn = rP[d].next()
                            TTn = rPT[d].next()
                            kb.tt("dve", Tn.v, pZ[:, 0:128], Tm.v, ALU.add)
                            yield
                            kb.tt("dve", TTn.v, pZ[:, 128:256], TTm.v, ALU.add)
                            yield
                            Tm, TTm = Tn, TTn
                        pX = pss.next()
                        kb.mm(pX[:, 0:256], TTm.v, R_.v)
                        yield
                        X = rX[d].next()
                        evac(X.v, pX[:, 0:256])
                        yield
                        pW = pss.next()
                        kb.transpose(pW[:, 0:128], X[:, 128:256], self.ident.v)
                        yield
                        wT = rwT[d].next()
                        evac(wT.v, pW[:, 0:128])
                        yield
                        kd = rkd[d].next()
                        kb.ts("pool", kd.v, ktm_t.v, EGL[:, n, col:col + 1])
                        yield
                        S_ = Sst[d]
                        p1 = pss.next()
                        kb.mm(p1[:, 0:128], wT.v, S_.v)
                        yield
                        vn = rvn[d].next()
                        kb.tt("dve", vn.v, X[:, 0:128], p1[:, 0:128], ALU.subtract)
                        yield
                        p2 = pss.next()
                        kb.mm(p2[:, 0:128], qd.v, S_.v, start=True, stop=False)
                        yield
                        kb.mm(p2[:, 0:128], AT.v, vn.v, start=False, stop=True)
                        yield
                        kb.tt("dve", oacc[:, n, :], oacc[:, n, :], p2[:, 0:128], ALU.add)
                        yield
                        p3 = pss.next()
                        kb.mm(p3[:, 0:128], kd.v, vn.v)
                        yield
                        kb.stt("dve", S_.v, S_.v, CD[:, n, col:col + 1], p3[:, 0:128], ALU.mult, ALU.add)
                        yield
                    for i in range(NT):
                        alive = [unit(d, i if d == 0 else NT - 1 - i) for d in _dirs]
                        while alive:
                            for g_ in list(alive):
                                try:
                                    next(g_)
                                except StopIteration:
                                    alive.remove(g_)
                    kb.f32r = False
                    if stop <= 5:
                        continue
                    for n in range(NT):
                        m_ = sm.next()
                        kb.act(sqj.v, oacc[:, n, :], AF.Square, accum_out=m_[:, 0:1])
                        kb.act(m_[:, 1:2], m_[:, 0:1], AF.Sqrt, scale=1.0 / 128.0, bias=self.epsT.v)
                        kb.recip(m_[:, 1:2], m_[:, 1:2])
                        on = onr.next()
                        kb.stt("dve", on.v, oacc[:, n, :], m_[:, 1:2], onormb.v, ALU.mult, ALU.mult)
                        p1 = pss.next()
                        if _os.environ.get("DN_RAW"):
                            kb.transpose(p1[:, 0:128], oacc[:, n, :], self.ident.v)
                            kb.copy("dve", oT[:, n * 128:(n + 1) * 128], p1[:, 0:128])
                            continue
                        kb.transpose(p1[:, 0:128], on.v, self.ident.v)
                        kb.tt("dve", oT[:, n * 128:(n + 1) * 128], p1[:, 0:128], zs[:, n * 128:(n + 1) * 128], ALU.mult)
                    kb.dma("pool", self.OT[hd, :, off:off + L], oT[:, :L])
            kb.barrier()


def _f32(a):
    return np.ascontiguousarray(np.asarray(a, dtype=np.float32))


def host_shared(inp, kinds):
    sh = {}
    sh["ada_w"] = _f32(inp["ada_w"])
    sh["ada_bT"] = _f32(np.asarray(inp["ada_b"]).reshape(4, 48, 128).transpose(0, 2, 1))
    sh["n1g"] = _f32(np.asarray(inp["norm1_g"]).reshape(4, 8, 128).transpose(0, 2, 1))
    sh["n2g"] = _f32(np.asarray(inp["norm2_g"]).reshape(4, 8, 128).transpose(0, 2, 1))
    sh["fing"] = _f32(np.asarray(inp["final_g"]).reshape(8, 128).T)
    for k in ("ffn_w1", "ffn_w3", "ffn_w2"):
        sh[k] = _f32(inp[k])
    sh["c_ident"] = np.eye(128, dtype=np.float32)
    i = np.arange(128)[:, None]
    j = np.arange(128)[None, :]
    sh["c_masks"] = _f32(np.stack([j <= i, j < i, j >= i, j > i]))
    if "da" in kinds:
        sh["da_w_in"] = _f32(inp["da_w_in"])
        sh["da_lam"] = _f32(np.asarray(inp["da_lam"]).reshape(1, 1, 256))
        sh["da_subln_g"] = _f32(np.asarray(inp["da_subln_g"]).reshape(1, 1, 128))
        sh["da_w_out"] = _f32(inp["da_w_out"])
        half = 32
        inv = np.power(np.float32(ROPE_THETA), -np.arange(half, dtype=np.float32) * np.float32(2.0) / np.float32(64))
        ang = np.arange(4096, dtype=np.float32)[None, :] * inv.astype(np.float32)[:, None]
        ang = np.tile(ang.astype(np.float32), (4, 1))
        sh["c_rope"] = _f32(np.stack([np.cos(ang), np.sin(ang)]))
    if "s5" in kinds:
        sh["s5_w_in"] = _f32(inp["s5_w_in"])
        sh["s5_w_glu"] = _f32(inp["s5_w_glu"])
        sh["s5_dT"] = _f32(np.asarray(inp["s5_d"]).reshape(8, 128).T)
        are = np.asarray(inp["s5_a_re"])[0]
        aim = np.asarray(inp["s5_a_im"])[0]
        ldt = np.asarray(inp["s5_log_dt"])[0]
        bre = np.asarray(inp["s5_b_re"])[0]
        bim = np.asarray(inp["s5_b_im"])[0]
        cre = np.asarray(inp["s5_c_re"])[0]
        cim = np.asarray(inp["s5_c_im"])[0]

        def layA(a):
            a = a.reshape(2, 8, 8, 1, 64)
            a = np.broadcast_to(a, (2, 8, 8, 16, 64))
            return a.transpose(0, 2, 3, 1, 4).reshape(2, 128, 8, 64)
        sh["s5_aA"] = _f32(np.stack([layA(are), layA(aim)]))
        d = ldt.reshape(2, 8, 8, 1)
        d = np.broadcast_to(d, (2, 8, 8, 16)).transpose(0, 2, 3, 1).reshape(2, 128, 8)
        sh["s5_dtA"] = _f32(d)

        def layBm(b):
            b = b.reshape(2, 8, 8, 64, 16)
            return b.transpose(0, 2, 4, 1, 3).reshape(2, 128, 8, 64)
        sh["s5_bA"] = _f32(np.stack([layBm(bre), layBm(bim)]))

        def layB(a):
            a = a.reshape(2, 32, 2, 64)
            return a.transpose(0, 2, 3, 1).reshape(2, 128, 32)
        sh["s5_aB"] = _f32(np.stack([layB(are), layB(aim)]))
        d = np.broadcast_to(ldt.reshape(2, 32, 2, 1), (2, 32, 2, 64)).transpose(0, 2, 3, 1).reshape(2, 128, 32)
        sh["s5_dtB"] = _f32(d)

        def layC(c):
            c = c.reshape(2, 32, 2, 16, 64)
            return c.transpose(0, 2, 4, 1, 3).reshape(2, 128, 32, 16)
        sh["s5_cB"] = _f32(np.stack([layC(cre), layC(cim)]))
        sh["c_rowmask"] = _f32((np.arange(128)[:, None] // 16) == np.arange(8)[None, :])
    if "na" in kinds:
        sh["na_w_in"] = _f32(inp["na_w_in"])
        sh["na_w_out"] = _f32(inp["na_w_out"])
        pad = np.zeros((1, 16 * 15 * 31 + 128), np.float32)
        pad[0, 64:64 + 16 * 15 * 31] = np.asarray(inp["na_rpb"], dtype=np.float32).reshape(-1)
        sh["c_antiI"] = _f32(np.eye(64)[::-1])
        sh["na_rpbp"] = pad
        q = np.arange(64)
        cs = np.clip(q - 8, 0, 48)
        kc = np.arange(128) % 64
        valid = (kc[:, None] >= cs[None, :]) & (kc[:, None] < cs[None, :] + 16)
        mm_ = np.broadcast_to(valid[:, None, :], (128, 14, 64))
        sh["c_namask"] = _f32(np.stack([mm_.astype(np.float32), np.where(mm_, 0.0, -30000.0)]))
    if "dn" in kinds:
        sh["dn_w_in"] = _f32(inp["dn_w_in"])
        cw = np.asarray(inp["dn_conv_w"])[0]
        sh["dn_convT"] = _f32(cw.reshape(4, 24, 128).transpose(2, 1, 0))
        sh["dn_alog"] = _f32(np.asarray(inp["dn_a_log"]).reshape(1, 16))
        sh["dn_dtb"] = _f32(np.asarray(inp["dn_dt_bias"]).reshape(1, 16))
        sh["dn_onorm"] = _f32(np.asarray(inp["dn_onorm_g"]).reshape(1, 128))
        sh["dn_w_out"] = _f32(inp["dn_w_out"])
        ci = np.arange(128)[:, None]
        si = np.arange(128)[None, :]
        lm = []
        for lev in range(7):
            h = 1 << lev
            lm.append((ci // (2 * h) == si // (2 * h)) & (ci % (2 * h) >= h) & (si % (2 * h) < h))
        lm = np.stack(lm).astype(np.float32)
        sh["c_lmask"] = _f32(np.stack([lm, lm.transpose(0, 2, 1)]))
    return sh


def host_core(inp, seq_list):
    xs, cs = [], []
    for grp, b in seq_list:
        x = np.asarray(inp["x_prompt" if grp == "p" else "x_sample"][b], dtype=np.float32)
        c = np.asarray(inp["c_prompt" if grp == "p" else "c_sample"][b], dtype=np.float32)
        xs.append(x.T)
        cs.append(c.reshape(8, 128).T)
    xT = np.concatenate(xs, axis=1).reshape(8, 128, -1)
    cT = np.stack(cs, axis=2)
    return {"xT": _f32(xT), "cT": _f32(cT)}


def run_cfg(inp, cfg, core_seq_lists):
    kinds = set(k for _, k in cfg["layers"])
    prog = Prog(cfg)
    sh = host_shared(inp, kinds)
    in_maps = []
    for sl in core_seq_lists:
        m = dict(sh)
        m.update(host_core(inp, sl))
        m = {k: v for k, v in m.items() if k in prog.din}
        in_maps.append(m)
    import os as _os
    if _os.environ.get("KTRACE"):
        res = run_bass_kernel_spmd(prog.nc, in_maps, core_ids=list(range(len(core_seq_lists))), trace=True)
        print("EXEC_TIME_NS", res.exec_time_ns, flush=True)
    else:
        res = run_bass_kernel_spmd(prog.nc, in_maps, core_ids=list(range(len(core_seq_lists))))
    outs = []
    prog.raw = res.results
    for ci, sl in enumerate(core_seq_lists):
        yT = np.asarray(res.results[ci]["yT"]).reshape(1024, -1)
        o = []
        off = 0
        for L in cfg["seqs"]:
            o.append(np.ascontiguousarray(yT[:, off:off + L].T))
            off += L
        outs.append(o)
    return outs, prog


FULL_LAYERS = [(0, "da"), (1, "s5"), (2, "na"), (3, "dn")]


def kernel(**inputs):
    B = 16
    cfg = {"seqs": [2048, 2048, 4096, 4096], "layers": FULL_LAYERS}
    lists = [[("p", 2 * c), ("p", 2 * c + 1), ("s", 2 * c), ("s", 2 * c + 1)] for c in range(8)]
    outs, _ = run_cfg(inputs, cfg, lists)
    yp = np.zeros((B, 2048, 1024), np.float32)
    ys = np.zeros((B, 4096, 1024), np.float32)
    for c in range(8):
        yp[2 * c], yp[2 * c + 1], ys[2 * c], ys[2 * c + 1] = outs[c]
    return (yp, ys)
```
